# Optimizing a Trainium2 kernel written in Bass

```python
import math
import numpy as np
import jax
import jax.numpy as jnp
from jax import lax

D_MODEL = 1024
BATCH = 2
SEQ = 16384
DEPTH = 4

CTX_LEN = 256
GRID_W = 64
N_MIXERS = 4
GROUP_W = D_MODEL // N_MIXERS
D_FF = 4 * D_MODEL
NORM_EPS = 1e-6

S5_CH = 16
S5_GROUPS = GROUP_W // S5_CH
S5_STATE = 64
S5_DT_MIN = 1e-3
S5_DT_MAX = 1e-1

ML_HEADS = 4
ML_DH = GROUP_W // ML_HEADS
ML_CHUNK = 128
ML_CONV = 3

DA_HEADS = 4
DA_DQK = GROUP_W // (2 * DA_HEADS)
DA_DV = 2 * DA_DQK
Q_BLOCK = 128
ROPE_BASE = 10000.0

GLA_HEADS = 4
GLA_DK = GROUP_W // (2 * GLA_HEADS)
GLA_DV = GROUP_W // GLA_HEADS
GLA_RANK = 16
GLA_TAU = 16.0
GLA_CHUNK = 16

IN_WIDTHS = (
    GROUP_W,
    GROUP_W, GROUP_W, GROUP_W, GROUP_W, 4 * ML_HEADS,
    GROUP_W, GROUP_W, GROUP_W,
    GLA_HEADS * GLA_DK, GLA_HEADS * GLA_DK, GROUP_W, GROUP_W, 2 * GLA_RANK,
)
IN_DIM = sum(IN_WIDTHS)

kernel_name = 'hymba_style_s5_mlstm_diffattn_gla_dit'

F32 = jnp.float32


def rms_norm(x, g):
    xf = x.astype(F32)
    y = xf * lax.rsqrt(jnp.mean(xf * xf, axis=-1, keepdims=True) + NORM_EPS)
    return y * g.astype(F32)


def split_cols(z):
    idx = np.cumsum(IN_WIDTHS)[:-1].tolist()
    return jnp.split(z, idx, axis=-1)


def to_heads(t, n_heads, dh):
    return t.astype(F32).reshape(t.shape[0], t.shape[1], n_heads, dh).transpose(0, 2, 1, 3)


def merge_heads(t):
    b, h, l, d = t.shape
    return t.transpose(0, 2, 1, 3).reshape(b, l, h * d)


def rev(t, d, axis):
    return jnp.flip(t, axis=axis) if d == 1 else t


def short_conv(x, w, b):
    pad = w.shape[0] // 2
    y = lax.conv_general_dilated(x, w[:, None, :].astype(x.dtype), window_strides=(1,), padding=[(pad, pad)],
                                 dimension_numbers=('NWC', 'WIO', 'NWC'), feature_group_count=x.shape[-1])
    return y + b


def axial_rope(n_tok, dim):
    rows = n_tok // GRID_W
    row = jnp.repeat(jnp.arange(rows, dtype=F32), GRID_W)
    col = jnp.tile(jnp.arange(GRID_W, dtype=F32), rows)
    n_freq = dim // 4
    inv = ROPE_BASE ** (-jnp.arange(n_freq, dtype=F32) / n_freq)
    ang = jnp.concatenate([row[:, None] * inv, col[:, None] * inv], axis=-1)
    return jnp.cos(ang), jnp.sin(ang)


def apply_rope(x, cos, sin):
    c = cos[None, :, None, None, :]
    s = sin[None, :, None, None, :]
    x1, x2 = x[..., 0::2], x[..., 1::2]
    return jnp.stack([x1 * c - x2 * s, x1 * s + x2 * c], axis=-1).reshape(x.shape)


def s5_discretize(lam_re, lam_im, log_step, b_re, b_im):
    lam_re, lam_im = lam_re.astype(F32), lam_im.astype(F32)
    b_re, b_im = b_re.astype(F32), b_im.astype(F32)
    dt = jnp.exp(log_step.astype(F32))[:, None]
    mag = jnp.exp(lam_re * dt)
    lb_re, lb_im = mag * jnp.cos(lam_im * dt), mag * jnp.sin(lam_im * dt)
    den = lam_re * lam_re + lam_im * lam_im
    fr = ((lb_re - 1.0) * lam_re + lb_im * lam_im) / den
    fi = (lb_im * lam_re - (lb_re - 1.0) * lam_im) / den
    bb_re = fr[..., None] * b_re - fi[..., None] * b_im
    bb_im = fr[..., None] * b_im + fi[..., None] * b_re
    return lb_re, lb_im, bb_re, bb_im


def complex_linear_combine(e1, e2):
    a1r, a1i, b1r, b1i = e1
    a2r, a2i, b2r, b2i = e2
    return (a2r * a1r - a2i * a1i, a2r * a1i + a2i * a1r,
            a2r * b1r - a2i * b1i + b2r, a2r * b1i + a2i * b1r + b2i)


def s5_scan(u, lb_re, lb_im, bb_re, bb_im, h0_re, h0_im):
    bu_re = jnp.einsum('blgh,gph->blgp', u, bb_re)
    bu_im = jnp.einsum('blgh,gph->blgp', u, bb_im)
    bu_re = bu_re.at[:, 0].add(lb_re * h0_re - lb_im * h0_im)
    bu_im = bu_im.at[:, 0].add(lb_re * h0_im + lb_im * h0_re)
    a_re = jnp.broadcast_to(lb_re, bu_re.shape)
    a_im = jnp.broadcast_to(lb_im, bu_im.shape)
    _, _, h_re, h_im = lax.associative_scan(complex_linear_combine, (a_re, a_im, bu_re, bu_im), axis=1)
    return h_re, h_im


def s5_readout(h_re, h_im, c_re, c_im):
    return (jnp.einsum('blgp,ghp->blgh', h_re, c_re.astype(F32))
            - jnp.einsum('blgp,ghp->blgh', h_im, c_im.astype(F32)))


def s5_direction(u_ctx, u_lat, lam_re, lam_im, log_step, b_re, b_im, c_re, c_im, d, need_ctx):
    lb_re, lb_im, bb_re, bb_im = s5_discretize(lam_re, lam_im, log_step, b_re, b_im)
    uc, ul = rev(u_ctx, d, 1), rev(u_lat, d, 1)
    zero = jnp.zeros((uc.shape[0], S5_GROUPS, S5_STATE), F32)
    hc_re, hc_im = s5_scan(uc, lb_re, lb_im, bb_re, bb_im, zero, zero)
    hl_re, hl_im = s5_scan(ul, lb_re, lb_im, bb_re, bb_im, hc_re[:, -1], hc_im[:, -1])
    y_lat = rev(s5_readout(hl_re, hl_im, c_re, c_im), d, 1)
    y_ctx = rev(s5_readout(hc_re, hc_im, c_re, c_im), d, 1) if need_ctx else None
    return y_ctx, y_lat


def s5_mixer(u_ctx, u_lat, p, need_ctx):
    def grouped(u):
        return u.astype(F32).reshape(u.shape[0], u.shape[1], S5_GROUPS, S5_CH)
    uc, ul = grouped(u_ctx), grouped(u_lat)
    outs = [s5_direction(uc, ul, p['s5_lam_re'][d], p['s5_lam_im'][d], p['s5_log_step'][d],
                         p['s5_b_re'][d], p['s5_b_im'][d], p['s5_c_re'][d], p['s5_c_im'][d], d, need_ctx)
            for d in range(2)]
    d_skip = p['s5_d'].reshape(S5_GROUPS, S5_CH)

    def glu(y, u):
        y = (y + d_skip * u).reshape(u.shape[0], u.shape[1], GROUP_W)
        g = jax.nn.gelu(y)
        return g * jax.nn.sigmoid(g @ p['s5_glu_w'] + p['s5_glu_b'])

    y_lat = glu(outs[0][1] + outs[1][1], ul)
    y_ctx = glu(outs[0][0] + outs[1][0], uc) if need_ctx else None
    return y_ctx, y_lat


def mlstm_chunked(q, k, v, ig, lf, state, chunk, need_out):
    bsz, nh, n_tok, dh = q.shape
    nc = n_tok // chunk
    qc = q.reshape(bsz, nh, nc, chunk, dh)
    kc = k.reshape(bsz, nh, nc, chunk, dh)
    vc = v.reshape(bsz, nh, nc, chunk, dh)
    igc = ig.reshape(bsz, nh, nc, chunk)
    bcum = jnp.cumsum(lf.reshape(bsz, nh, nc, chunk), axis=-1)
    btot = bcum[..., -1]
    w = btot[..., None] - bcum + igc
    g = jnp.max(w, axis=-1)
    e = jnp.exp(w - g[..., None])
    c_chunk = jnp.einsum('bhcs,bhcsv,bhcsk->bhcvk', e, vc, kc)
    n_chunk = jnp.einsum('bhcs,bhcsk->bhck', e, kc)

    def step(carry, inp):
        c_st, n_st, m_st = carry
        bt, gt, ct, nt = inp
        m_new = jnp.maximum(bt + m_st, gt)
        a = jnp.exp(bt + m_st - m_new)
        s = jnp.exp(gt - m_new)
        c_new = a[..., None, None] * c_st + s[..., None, None] * ct
        n_new = a[..., None] * n_st + s[..., None] * nt
        return (c_new, n_new, m_new), (c_st, n_st, m_st)

    mv = lambda t: jnp.moveaxis(t, 2, 0)
    final, (c_prev, n_prev, m_prev) = lax.scan(step, state, (mv(btot), mv(g), mv(c_chunk), mv(n_chunk)))
    if not need_out:
        return None, final
    c_prev = jnp.moveaxis(c_prev, 0, 2)
    n_prev = jnp.moveaxis(n_prev, 0, 2)
    m_prev = jnp.moveaxis(m_prev, 0, 2)

    lower = jnp.tril(jnp.ones((chunk, chunk), bool))
    dmat = jnp.where(lower, bcum[..., :, None] - bcum[..., None, :] + igc[..., None, :], -jnp.inf)
    inter = bcum + m_prev[..., None]
    m_t = jnp.maximum(inter, jnp.max(dmat, axis=-1))
    sc = jnp.einsum('bhctd,bhcsd->bhcts', qc, kc) * jnp.exp(dmat - m_t[..., None])
    a_int = jnp.exp(inter - m_t)
    num = (jnp.einsum('bhcts,bhcsv->bhctv', sc, vc)
           + a_int[..., None] * jnp.einsum('bhcvk,bhctk->bhctv', c_prev, qc))
    den = jnp.sum(sc, axis=-1) + a_int * jnp.einsum('bhck,bhctk->bhct', n_prev, qc)
    h = num / jnp.maximum(jnp.abs(den), jnp.exp(-m_t))[..., None]
    return h.reshape(bsz, nh, n_tok, dh), final


def mlstm_prep(q_in, k_in, v, gates, p):
    bsz, n_tok = q_in.shape[:2]
    qk = jax.nn.silu(short_conv(jnp.concatenate([q_in, k_in], axis=-1).astype(F32), p['ml_conv_w'], p['ml_conv_b']))
    q, k = jnp.split(qk, 2, axis=-1)
    gt = (gates + p['ml_gate_b']).astype(F32).reshape(bsz, n_tok, 4, ML_HEADS).transpose(0, 2, 3, 1)
    ig = gt[:, 0::2]
    lf = jax.nn.log_sigmoid(gt[:, 1::2])
    return (to_heads(q, ML_HEADS, ML_DH), to_heads(k, ML_HEADS, ML_DH) * ML_DH ** -0.5,
            to_heads(v, ML_HEADS, ML_DH), ig, lf)


def mlstm_mixer(zc, zl, p, need_ctx):
    cq, ck, cv, cig, clf = mlstm_prep(zc[0], zc[1], zc[2], zc[4], p)
    lq, lk, lv, lig, llf = mlstm_prep(zl[0], zl[1], zl[2], zl[4], p)
    bsz = lq.shape[0]
    zero = (jnp.zeros((bsz, ML_HEADS, ML_DH, ML_DH), F32), jnp.zeros((bsz, ML_HEADS, ML_DH), F32),
            jnp.zeros((bsz, ML_HEADS), F32))
    h_lat, h_ctx = 0.0, 0.0
    for d in range(2):
        hc, st = mlstm_chunked(rev(cq, d, 2), rev(ck, d, 2), rev(cv, d, 2), rev(cig[:, d], d, 2),
                               rev(clf[:, d], d, 2), zero, ML_CHUNK, need_ctx)
        hl, _ = mlstm_chunked(rev(lq, d, 2), rev(lk, d, 2), rev(lv, d, 2), rev(lig[:, d], d, 2),
                              rev(llf[:, d], d, 2), st, ML_CHUNK, True)
        h_lat = h_lat + rev(hl, d, 2)
        if need_ctx:
            h_ctx = h_ctx + rev(hc, d, 2)
    gain = p['ml_norm_g'].reshape(ML_HEADS, 1, ML_DH)

    def finish(h, o):
        return jax.nn.sigmoid(o.astype(F32)) * merge_heads(rms_norm(h, gain))

    y_ctx = finish(h_ctx, zc[3]) if need_ctx else None
    return y_ctx, finish(h_lat, zl[3])


def diff_attention_mixer(zc, zl, p, layer_idx, need_ctx):
    def heads_qk(t):
        return t.astype(F32).reshape(t.shape[0], t.shape[1], DA_HEADS, 2, DA_DQK)

    def heads_v(t):
        return t.astype(F32).reshape(t.shape[0], t.shape[1], DA_HEADS, DA_DV)

    scale = DA_DQK ** -0.5
    bsz, n_tok = zl[0].shape[:2]
    cos, sin = axial_rope(n_tok, DA_DQK)
    ql = apply_rope(heads_qk(zl[0]), cos, sin) * scale
    kl = apply_rope(heads_qk(zl[1]), cos, sin)
    vl = heads_v(zl[2])
    qc, kc, vc = heads_qk(zc[0]) * scale, heads_qk(zc[1]), heads_v(zc[2])
    lam_init = 0.8 - 0.6 * math.exp(-0.3 * layer_idx)
    lp = p['da_lam'].astype(F32)
    lam = jnp.exp(jnp.sum(lp[0] * lp[1])) - jnp.exp(jnp.sum(lp[2] * lp[3])) + lam_init
    k_all = jnp.concatenate([kc, kl], axis=1)
    v_all = jnp.concatenate([vc, vl], axis=1)

    def attend(qb, keys, vals):
        s = jnp.einsum('bqhmd,bkhmd->bhmqk', qb, keys)
        pr = jax.nn.softmax(s, axis=-1)
        w = pr[:, :, 0] - lam * pr[:, :, 1]
        return jnp.einsum('bhqk,bkhv->bqhv', w, vals)

    nb = n_tok // Q_BLOCK
    qblocks = jnp.moveaxis(ql.reshape(bsz, nb, Q_BLOCK, DA_HEADS, 2, DA_DQK), 1, 0)
    o_l = lax.map(lambda qb: attend(qb, k_all, v_all), qblocks)
    o_l = jnp.moveaxis(o_l, 0, 1).reshape(bsz, n_tok, DA_HEADS, DA_DV)

    def finish(o):
        o = rms_norm(o, p['da_norm_g']) * (1.0 - lam_init)
        return o.reshape(o.shape[0], o.shape[1], GROUP_W)

    y_ctx = finish(attend(qc, kc, vc)) if need_ctx else None
    return y_ctx, finish(o_l)


def gla_chunked(q, k, v, la, s0, chunk, need_out):
    bsz, nh, n_tok, dk = q.shape
    dv = v.shape[-1]
    nc = n_tok // chunk
    qc = q.reshape(bsz, nh, nc, chunk, dk)
    kc = k.reshape(bsz, nh, nc, chunk, dk)
    vc = v.reshape(bsz, nh, nc, chunk, dv)
    bcum = jnp.cumsum(la.reshape(bsz, nh, nc, chunk, dk), axis=3)
    btot = bcum[:, :, :, -1]
    s_chunk = jnp.einsum('bhctk,bhctv->bhckv', kc * jnp.exp(btot[:, :, :, None] - bcum), vc)

    def step(s, inp):
        dec, sc = inp
        return jnp.exp(dec)[..., None] * s + sc, s

    s_final, s_prev = lax.scan(step, s0, (jnp.moveaxis(btot, 2, 0), jnp.moveaxis(s_chunk, 2, 0)))
    if not need_out:
        return None, s_final
    s_prev = jnp.moveaxis(s_prev, 0, 2)
    o_inter = jnp.einsum('bhctk,bhckv->bhctv', qc * jnp.exp(bcum), s_prev)
    lower = jnp.tril(jnp.ones((chunk, chunk), bool))[:, :, None]
    decay = jnp.exp(jnp.where(lower, bcum[:, :, :, :, None, :] - bcum[:, :, :, None, :, :], -jnp.inf))
    att = jnp.einsum('bhctk,bhcsk,bhctsk->bhcts', qc, kc, decay)
    o = o_inter + jnp.einsum('bhcts,bhcsv->bhctv', att, vc)
    return o.reshape(bsz, nh, n_tok, dv), s_final


def gla_prep(q, k, v, a_lr, p):
    bsz, n_tok = q.shape[:2]
    a_lr = a_lr.astype(F32).reshape(bsz, n_tok, 2, GLA_RANK)
    la = jax.nn.log_sigmoid(jnp.einsum('bldr,drk->bldk', a_lr, p['gla_alpha_w']) + p['gla_alpha_b']) / GLA_TAU
    la = la.reshape(bsz, n_tok, 2, GLA_HEADS, GLA_DK).transpose(2, 0, 3, 1, 4)
    return (to_heads(q, GLA_HEADS, GLA_DK) * GLA_DK ** -0.5, to_heads(k, GLA_HEADS, GLA_DK),
            to_heads(v, GLA_HEADS, GLA_DV), la)


def gla_mixer(zc, zl, p, need_ctx):
    cq, ck, cv, cla = gla_prep(zc[0], zc[1], zc[2], zc[4], p)
    lq, lk, lv, lla = gla_prep(zl[0], zl[1], zl[2], zl[4], p)
    zero = jnp.zeros((lq.shape[0], GLA_HEADS, GLA_DK, GLA_DV), F32)
    o_lat, o_ctx = 0.0, 0.0
    for d in range(2):
        oc, st = gla_chunked(rev(cq, d, 2), rev(ck, d, 2), rev(cv, d, 2), rev(cla[d], d, 2), zero, GLA_CHUNK, need_ctx)
        ol, _ = gla_chunked(rev(lq, d, 2), rev(lk, d, 2), rev(lv, d, 2), rev(lla[d], d, 2), st, GLA_CHUNK, True)
        o_lat = o_lat + rev(ol, d, 2)
        if need_ctx:
            o_ctx = o_ctx + rev(oc, d, 2)

    def finish(o, r):
        return merge_heads(rms_norm(o, p['gla_norm_g'])) * jax.nn.silu(r.astype(F32))

    y_ctx = finish(o_ctx, zc[3]) if need_ctx else None
    return y_ctx, finish(o_lat, zl[3])


def sq_relu_mlp(h, w1, w2):
    return jnp.square(jax.nn.relu(h @ w1)) @ w2


def hybrid_layer(x_lat, x_ctx, mod_lat, mod_ctx, p, layer_idx, need_ctx):
    sh1, sc1, g1, sh2, sc2, g2 = jnp.split(mod_lat[:, None, :], 6, axis=-1)
    csh1, csc1, cg1, csh2, csc2, cg2 = jnp.split(mod_ctx, 6, axis=-1)
    hl = rms_norm(x_lat, p['norm1_g']) * (1.0 + sc1) + sh1
    hc = rms_norm(x_ctx, p['norm1_g']) * (1.0 + csc1) + csh1
    zl = split_cols(hl @ p['w_in'])
    zc = split_cols(hc @ p['w_in'])
    s5 = s5_mixer(zc[0], zl[0], p, need_ctx)
    ml = mlstm_mixer(zc[1:6], zl[1:6], p, need_ctx)
    da = diff_attention_mixer(zc[6:9], zl[6:9], p, layer_idx, need_ctx)
    gl = gla_mixer(zc[9:14], zl[9:14], p, need_ctx)
    mix_l = jnp.concatenate([s5[1], ml[1], da[1], gl[1]], axis=-1) @ p['w_out']
    x_lat = x_lat + g1 * mix_l
    hl2 = rms_norm(x_lat, p['norm2_g']) * (1.0 + sc2) + sh2
    x_lat = x_lat + g2 * sq_relu_mlp(hl2, p['mlp_w1'], p['mlp_w2'])
    if need_ctx:
        mix_c = jnp.concatenate([s5[0], ml[0], da[0], gl[0]], axis=-1) @ p['w_out']
        x_ctx = x_ctx + cg1 * mix_c
        hc2 = rms_norm(x_ctx, p['norm2_g']) * (1.0 + csc2) + csh2
        x_ctx = x_ctx + cg2 * sq_relu_mlp(hc2, p['mlp_w1'], p['mlp_w2'])
    return x_lat, x_ctx


def setup_inputs(seed: int = 0) -> dict:
    key = jax.random.key(seed)
    ks = list(jax.random.split(key, 40))
    cnt = [0]

    def nk():
        cnt[0] += 1
        return ks[cnt[0] - 1]

    def nrm(shape, scale):
        return jax.random.normal(nk(), shape, F32) * scale

    L = DEPTH
    G, P = S5_GROUPS, S5_STATE
    f_bias = jnp.linspace(3.0, 6.0, ML_HEADS, dtype=F32)
    log_dt = (jax.random.uniform(nk(), (L, 2, G), F32) * (math.log(S5_DT_MAX) - math.log(S5_DT_MIN))
              + math.log(S5_DT_MIN))
    return {
        'x': nrm((BATCH, SEQ, D_MODEL), 1.0),
        'c': nrm((BATCH, D_MODEL), 1.0),
        'ctx': nrm((BATCH, CTX_LEN, D_MODEL), 1.0),
        'c_ctx': nrm((D_MODEL,), 1.0),
        'ada_w': nrm((L, D_MODEL, 6 * D_MODEL), 0.5 * D_MODEL ** -0.5),
        'ada_b': nrm((L, 6 * D_MODEL), 0.02),
        'norm1_g': 1.0 + nrm((L, D_MODEL), 0.02),
        'norm2_g': 1.0 + nrm((L, D_MODEL), 0.02),
        'w_in': nrm((L, D_MODEL, IN_DIM), D_MODEL ** -0.5),
        's5_lam_re': -0.5 + nrm((L, 2, G, P), 0.01),
        's5_lam_im': math.pi * jnp.arange(P, dtype=F32) + nrm((L, 2, G, P), 0.01),
        's5_log_step': log_dt,
        's5_b_re': nrm((L, 2, G, P, S5_CH), (2 * S5_CH) ** -0.5),
        's5_b_im': nrm((L, 2, G, P, S5_CH), (2 * S5_CH) ** -0.5),
        's5_c_re': nrm((L, 2, G, S5_CH, P), P ** -0.5),
        's5_c_im': nrm((L, 2, G, S5_CH, P), P ** -0.5),
        's5_d': nrm((L, GROUP_W), 1.0),
        's5_glu_w': nrm((L, GROUP_W, GROUP_W), GROUP_W ** -0.5),
        's5_glu_b': nrm((L, GROUP_W), 0.02),
        'ml_conv_w': nrm((L, ML_CONV, 2 * GROUP_W), ML_CONV ** -0.5),
        'ml_conv_b': nrm((L, 2 * GROUP_W), 0.02),
        'ml_gate_b': jnp.concatenate([nrm((L, ML_HEADS), 0.1), f_bias + nrm((L, ML_HEADS), 0.1),
                                      nrm((L, ML_HEADS), 0.1), f_bias + nrm((L, ML_HEADS), 0.1)], axis=-1),
        'ml_norm_g': 1.0 + nrm((L, GROUP_W), 0.02),
        'da_lam': nrm((L, 4, DA_DQK), 0.1),
        'da_norm_g': 1.0 + nrm((L, DA_DV), 0.02),
        'gla_alpha_w': nrm((L, 2, GLA_RANK, GLA_HEADS * GLA_DK), GLA_RANK ** -0.5),
        'gla_alpha_b': nrm((L, 2, GLA_HEADS * GLA_DK), 0.1),
        'gla_norm_g': 1.0 + nrm((L, GLA_DV), 0.02),
        'w_out': nrm((L, D_MODEL, D_MODEL), D_MODEL ** -0.5),
        'mlp_w1': nrm((L, D_MODEL, D_FF), D_MODEL ** -0.5),
        'mlp_w2': nrm((L, D_FF, D_MODEL), D_FF ** -0.5),
        'final_norm_g': 1.0 + nrm((D_MODEL,), 0.02),
    }


def reference(x, c, ctx, c_ctx, ada_w, ada_b, norm1_g, norm2_g, w_in, s5_lam_re, s5_lam_im, s5_log_step,
              s5_b_re, s5_b_im, s5_c_re, s5_c_im, s5_d, s5_glu_w, s5_glu_b, ml_conv_w, ml_conv_b, ml_gate_b,
              ml_norm_g, da_lam, da_norm_g, gla_alpha_w, gla_alpha_b, gla_norm_g, w_out, mlp_w1, mlp_w2,
              final_norm_g):
    x_lat, x_ctx = x, ctx
    for l in range(DEPTH):
        p = dict(norm1_g=norm1_g[l], norm2_g=norm2_g[l], w_in=w_in[l],
                 s5_lam_re=s5_lam_re[l], s5_lam_im=s5_lam_im[l], s5_log_step=s5_log_step[l],
                 s5_b_re=s5_b_re[l], s5_b_im=s5_b_im[l], s5_c_re=s5_c_re[l], s5_c_im=s5_c_im[l],
                 s5_d=s5_d[l], s5_glu_w=s5_glu_w[l], s5_glu_b=s5_glu_b[l],
                 ml_conv_w=ml_conv_w[l], ml_conv_b=ml_conv_b[l], ml_gate_b=ml_gate_b[l], ml_norm_g=ml_norm_g[l],
                 da_lam=da_lam[l], da_norm_g=da_norm_g[l],
                 gla_alpha_w=gla_alpha_w[l], gla_alpha_b=gla_alpha_b[l], gla_norm_g=gla_norm_g[l],
                 w_out=w_out[l], mlp_w1=mlp_w1[l], mlp_w2=mlp_w2[l])
        mod_lat = jax.nn.silu(c) @ ada_w[l] + ada_b[l]
        mod_ctx = jax.nn.silu(c_ctx) @ ada_w[l] + ada_b[l]
        x_lat, x_ctx = hybrid_layer(x_lat, x_ctx, mod_lat, mod_ctx, p, l, l < DEPTH - 1)
    return rms_norm(x_lat, final_norm_g)
```

```python
import math
import numpy as np
from contextlib import ExitStack
import concourse.bass as bass
import concourse.mybir as mybir
from concourse.bass_utils import run_bass_kernel_spmd

F32 = mybir.dt.float32
BF16 = mybir.dt.bfloat16
AF = mybir.ActivationFunctionType
ALU = mybir.AluOpType
AX = mybir.AxisListType

D = 1024
CTX = 256
DFF = 4096
IN_DIM = 2864
EPS = 1e-6


class Buf:
    __slots__ = ("name", "w", "r")

    def __init__(self, name=""):
        self.name = name
        self.w = None
        self.r = {}


class KB:
    ENG = ("pe", "act", "dve", "pool", "sp")

    def __init__(self, nc, es, ndma=8):
        self.nc = nc
        self.sem = {k: es.enter_context(nc.semaphore("s_" + k)) for k in self.ENG}
        self.cnt = {k: 0 for k in self.ENG}
        self.waited = {k: {} for k in self.ENG}
        self.dsem = [es.enter_context(nc.semaphore(f"d_{i}")) for i in range(ndma)]
        self.dcnt = [0] * ndma
        self.dnext = 0
        self.ninst = 0
        self.prog = {k: [] for k in self.ENG}

    def emit(self):
        self.barrier()
        nc = self.nc
        prog = self.prog
        self.prog = {k: [] for k in self.ENG}
        with nc.Block() as block:
            def mk(k):
                def f(e):
                    for c in prog[k]:
                        c(e)
                return f
            block.tensor(mk("pe"))
            block.scalar(mk("act"))
            block.vector(mk("dve"))
            block.gpsimd(mk("pool"))
            block.sync(mk("sp"))

    def _wait(self, eng, key, semh, val):
        if val <= 0:
            return
        w = self.waited[eng]
        if w.get(key, 0) >= val:
            return
        self.prog[eng].append(lambda e: e.wait_ge(semh, val))
        self.ninst += 1
        w[key] = val

    def _deps(self, eng, reads, writes):
        deps = {}

        def add(t):
            if t is None:
                return
            k = t[0]
            if k not in deps or deps[k][2] < t[2]:
                deps[k] = t
        for b in reads:
            add(b.w)
        for b in writes:
            add(b.w)
            for t in b.r.values():
                add(t)
        for k, (key, semh, val) in deps.items():
            self._wait(eng, key, semh, val)

    def op(self, eng, fn, reads=(), writes=()):
        self._deps(eng, reads, writes)
        sem = self.sem[eng]
        self.prog[eng].append(lambda e: fn(e).then_inc(sem, 1))
        self.cnt[eng] += 1
        self.ninst += 1
        t = (eng, sem, self.cnt[eng])
        for b in reads:
            b.r[eng] = t
        for b in writes:
            b.w = t
            b.r = {}

    def dma(self, out, in_, reads=(), writes=(), eng="sp", **kw):
        self._deps(eng, reads, writes)
        i = self.dnext
        self.dnext = (self.dnext + 1) % len(self.dsem)
        key = ("d", i)
        self._wait(eng, key, self.dsem[i], self.dcnt[i])
        dsem = self.dsem[i]
        self.prog[eng].append(lambda e: e.dma_start(out=out, in_=in_, **kw).then_inc(dsem, 16))
        self.dcnt[i] += 16
        self.ninst += 1
        t = (key, dsem, self.dcnt[i])
        for b in reads:
            b.r[key] = t
        for b in writes:
            b.w = t
            b.r = {}

    def barrier(self):
        for eng in self.ENG:
            for k in self.ENG:
                if k != eng:
                    self._wait(eng, k, self.sem[k], self.cnt[k])
            for i in range(len(self.dsem)):
                self._wait(eng, ("d", i), self.dsem[i], self.dcnt[i])


class T:
    def __init__(self, t, name="", psum=False):
        self.t = t
        self.b = Buf(name)
        self.psum = psum

    def __getitem__(self, k):
        return self.t[k]


class Ctx:
    pass


def host_consts(L):
    Ttok = CTX + L
    ident = np.eye(128, dtype=np.float32)
    jrev = np.ascontiguousarray(ident[::-1])
    s = np.arange(128)
    triu = (s[:, None] <= s[None, :]).astype(np.float32)
    tril = (s[:, None] >= s[None, :]).astype(np.float32)
    nf = 8
    inv = (10000.0 ** (-np.arange(nf, dtype=np.float32) / nf)).astype(np.float32)
    t = np.arange(L)
    row = (t // 64).astype(np.float32)
    col = (t % 64).astype(np.float32)
    ang = np.concatenate([row[:, None] * inv, col[:, None] * inv], axis=-1).astype(np.float32)
    cos = np.concatenate([np.ones((CTX, 16), np.float32), np.cos(ang).astype(np.float32)], 0)
    sin = np.concatenate([np.zeros((CTX, 16), np.float32), np.sin(ang).astype(np.float32)], 0)
    iota = np.tile(np.arange(1, 513, dtype=np.float32)[None, :], (128, 1))
    return dict(c_ident=ident, c_jrev=jrev, c_triu=triu, c_tril=tril,
                c_cos=np.ascontiguousarray(cos), c_sin=np.ascontiguousarray(sin), c_iota=iota)


PARAM_SHAPES = dict(
    ada_w=(1024, 6144), ada_b=(6144,), norm1_g=(1024,), norm2_g=(1024,), w_in=(1024, IN_DIM),
    s5_lam_re=(2, 16, 64), s5_lam_im=(2, 16, 64), s5_log_step=(2, 16), s5_b_re=(2, 16, 64, 16),
    s5_b_im=(2, 16, 64, 16), s5_c_re=(2, 16, 16, 64), s5_c_im=(2, 16, 16, 64), s5_d=(256,),
    s5_glu_w=(256, 256), s5_glu_b=(256,), ml_conv_w=(3, 512), ml_conv_b=(512,), ml_gate_b=(16,),
    ml_norm_g=(256,), da_lam=(4, 32), da_norm_g=(64,), gla_alpha_w=(2, 16, 128), gla_alpha_b=(2, 128),
    gla_norm_g=(64,), w_out=(1024, 1024), mlp_w1=(1024, 4096), mlp_w2=(4096, 1024))


def blocks_of(Ttok, nb=512):
    out = [(0, CTX, True)]
    t = CTX
    while t < Ttok:
        n = min(nb, Ttok - t)
        out.append((t, n, False))
        t += n
    return out


class Ops:
    def __init__(self, kb):
        self.kb = kb

    @staticmethod
    def _b(ts):
        return [t.b if isinstance(t, T) else t for t in ts]

    @staticmethod
    def _rw(r, w):
        rr = [t.b for t in r if not t.psum]
        ww = [t.b for t in w] + [t.b for t in r if t.psum]
        return rr, ww

    def mm(self, out, lhsT, rhs, start, stop, r, w):
        self.kb.op("pe", lambda e: e.matmul(out, lhsT=lhsT, rhs=rhs, start=start, stop=stop), *self._rw(r, w))

    def tr(self, out, in_, ident, r, w):
        self.kb.op("pe", lambda e: e.transpose(out, in_, ident), *self._rw(r, w))

    def act(self, out, in_, func, r, w, bias=0.0, scale=1.0):
        self.kb.op("act", lambda e: e.activation(out=out, in_=in_, func=func, bias=bias, scale=scale),
                   *self._rw(r, w))

    def tt(self, eng, out, in0, in1, op, r, w):
        self.kb.op(eng, lambda e: e.tensor_tensor(out=out, in0=in0, in1=in1, op=op), *self._rw(r, w))

    def ts(self, eng, out, in0, s1, s2, op0, op1, r, w):
        if s2 is None:
            self.kb.op(eng, lambda e: e.tensor_scalar(out=out, in0=in0, scalar1=s1, scalar2=None, op0=op0),
                       *self._rw(r, w))
        else:
            self.kb.op(eng, lambda e: e.tensor_scalar(out=out, in0=in0, scalar1=s1, scalar2=s2, op0=op0, op1=op1),
                       *self._rw(r, w))

    def stt(self, eng, out, in0, scalar, in1, op0, op1, r, w):
        self.kb.op(eng, lambda e: e.scalar_tensor_tensor(out=out, in0=in0, scalar=scalar, in1=in1, op0=op0, op1=op1),
                   *self._rw(r, w))

    def copy(self, eng, out, in_, r, w):
        if eng == "act":
            self.kb.op("act", lambda e: e.copy(out=out, in_=in_), *self._rw(r, w))
        else:
            self.kb.op(eng, lambda e: e.tensor_copy(out=out, in_=in_), *self._rw(r, w))

    def memset(self, eng, out, val, w):
        self.kb.op(eng, lambda e: e.memset(out, val), *self._rw([], w))

    def scan(self, out, d0, d1, init, op0, op1, r, w):
        self.kb.op("dve", lambda e: e.tensor_tensor_scan(out=out, data0=d0, data1=d1, initial=init, op0=op0, op1=op1),
                   *self._rw(r, w))

    def red(self, out, in_, op, r, w):
        self.kb.op("dve", lambda e: e.tensor_reduce(out=out, in_=in_, axis=AX.X, op=op), *self._rw(r, w))

    def recip(self, out, in_, r, w):
        self.kb.op("dve", lambda e: e.reciprocal(out=out, in_=in_), *self._rw(r, w))

    def dma(self, out, in_, r, w, eng="sp", **kw):
        rr, ww = self._rw(r, w)
        self.kb.dma(out, in_, rr, ww, eng=eng, **kw)


class Phase:
    def __init__(self, g):
        self.g = g
        self.es = ExitStack()
        self.n = 0

    def __enter__(self):
        self.es.__enter__()
        return self

    def __exit__(self, *a):
        self.g.kb.emit()
        return self.es.__exit__(*a)

    def sb(self, shape, dt=F32, name="t"):
        self.n += 1
        self.g.uid += 1
        t = self.es.enter_context(self.g.nc.sbuf_tensor(f"{name}_{self.g.uid}", list(shape), dt))
        return T(t, name)


class G:
    pass


def build(L, depth, dbg=()):
    Ttok = CTX + L
    nc = bass.Bass("TRN2", target_bir_lowering=False)
    g = G()
    g.nc = nc
    g.uid = 0
    g.L, g.Ttok, g.depth = L, Ttok, depth
    es = ExitStack()
    g.kb = KB(nc, es)
    g.o = Ops(g.kb)
    o = g.o

    def din(name, shape, dt=F32):
        return nc.dram_tensor(name, list(shape), dt, kind="ExternalInput").ap()

    def dscr(name, shape, dt=F32):
        kind = "ExternalOutput" if name in dbg else "Internal"
        return nc.dram_tensor(name, list(shape), dt, kind=kind).ap()

    g.xin = din("xT", [D, Ttok])
    g.cvec = din("cvec", [D, 2])
    g.p = {k: din(k, (depth,) + v) for k, v in PARAM_SHAPES.items()}
    g.fng = din("final_norm_g", [D])
    g.c = {k: din(k, v.shape) for k, v in host_consts(L).items()}
    g.yT = nc.dram_tensor("yT", [D, L], F32, kind="ExternalOutput").ap()
    g.XT = dscr("XT", [D, Ttok])
    g.X1T = dscr("X1T", [D, Ttok])
    g.H2T = dscr("H2T", [D, Ttok], BF16)
    g.CAT = dscr("CAT", [D, Ttok], BF16)
    g.Y5T = dscr("Y5T", [256, Ttok])
    g.U32T = dscr("U32T", [256, Ttok])
    g.UTOK = dscr("UTOK", [Ttok, 256], BF16)
    g.MLQKT = dscr("MLQKT", [512, Ttok])
    g.MLV = dscr("MLV", [Ttok, 256], BF16)
    g.MLO = dscr("MLO", [Ttok, 256])
    g.MLG = dscr("MLG", [Ttok, 16])
    g.DAQ = dscr("DAQ", [Ttok, 256])
    g.DAK = dscr("DAK", [Ttok, 256])
    g.DAV = dscr("DAV", [Ttok, 256], BF16)
    g.GLQT = dscr("GLQT", [128, Ttok], BF16)
    g.GLKT = dscr("GLKT", [128, Ttok], BF16)
    g.GLAT = dscr("GLAT", [32, Ttok], BF16)
    g.GLK = dscr("GLK", [Ttok, 128], BF16)
    g.GLV = dscr("GLV", [Ttok, 256], BF16)
    g.GLR = dscr("GLR", [Ttok, 256])

    g.ps = [T(es.enter_context(nc.psum_tensor(f"ps{i}", [128, 512], F32)), f"ps{i}", True) for i in range(8)]
    g.psn = 0

    def psb(shape, dt=F32, name="c"):
        g.uid += 1
        return T(es.enter_context(nc.sbuf_tensor(f"{name}_{g.uid}", list(shape), dt)), name)
    g.ident = psb([128, 128], F32, "ident")
    g.jrev = psb([128, 128], F32, "jrev")
    g.triu = psb([128, 128], F32, "triu")
    g.tril = psb([128, 128], F32, "tril")
    g.ones_bf = psb([128, 128], BF16, "ones_bf")
    g.ones_f = psb([128, 128], F32, "ones_f")
    g.modT = psb([128, 48, 2], F32, "modT")
    g.A1 = psb([128, 8, 2], F32, "A1")
    g.A2 = psb([128, 8, 2], F32, "A2")
    o.dma(g.ident[:], g.c["c_ident"][:, :], [], [g.ident])
    o.dma(g.jrev[:], g.c["c_jrev"][:, :], [], [g.jrev])
    o.dma(g.triu[:], g.c["c_triu"][:, :], [], [g.triu])
    o.dma(g.tril[:], g.c["c_tril"][:, :], [], [g.tril])
    o.memset("dve", g.ones_bf[:], 1.0, [g.ones_bf])
    o.memset("dve", g.ones_f[:], 1.0, [g.ones_f])

    for l in range(depth):
        last = (l == depth - 1)
        xsrc = g.xin if l == 0 else g.XT
        import os
        stop = os.environ.get("KSTOP", "")
        phase_mod(g, l)
        if stop == "mod":
            break
        phase_p1(g, l, xsrc)
        if stop == "p1":
            break
        phase_mixers(g, l)
        if stop == "mix":
            break
        phase_p3a(g, l, xsrc, last)
        if stop == "p3a":
            break
        phase_p3b(g, l, last)
    g.kb.emit()
    es.close()
    return nc


def next_ps(g):
    t = g.ps[g.psn]
    g.psn = (g.psn + 1) % 8
    return t


def load_cols(g, ph, vec_ap, n, dst, dcol0=0):
    o = g.o
    nj = n // 128
    rows = ph.sb([nj, 128], F32, "lc_rows")
    o.dma(rows[:], vec_ap.rearrange("(j p) -> j p", p=128), [], [rows])
    ps = next_ps(g)
    o.tr(ps[:, 0:nj], rows[:], g.ident[0:nj, 0:nj], [rows, g.ident], [ps])
    o.copy("dve", dst[:, dcol0:dcol0 + nj], ps[:, 0:nj], [ps], [dst])


def phase_mod(g, l):
    o = g.o
    with Phase(g) as ph:
        cv = ph.sb([128, 8, 2], F32, "cv")
        sc = ph.sb([128, 8, 2], F32, "sc")
        bias = ph.sb([128, 48], F32, "bias")
        gn = ph.sb([128, 16], F32, "gn")
        o.dma(cv[:], g.cvec.rearrange("(fc p) two -> p fc two", p=128), [], [cv])
        o.act(sc[:], cv[:], AF.Silu, [cv], [sc])
        load_cols(g, ph, g.p["ada_b"][l], 6144, bias)
        load_cols(g, ph, g.p["norm1_g"][l], 1024, gn, 0)
        load_cols(g, ph, g.p["norm2_g"][l], 1024, gn, 8)
        wbuf = [ph.sb([128, 8, 1024], F32, "adaw") for _ in range(2)]
        for m in range(6):
            wb = wbuf[m % 2]
            for kc in range(8):
                o.dma(wb[:, kc, :], g.p["ada_w"][l][kc * 128:(kc + 1) * 128, m * 1024:(m + 1) * 1024], [], [wb])
            for fc in range(8):
                ps = next_ps(g)
                for kc in range(8):
                    o.mm(ps[:, 0:2], wb[:, kc, fc * 128:(fc + 1) * 128], sc[:, kc, :], kc == 0, kc == 7, [wb, sc], [ps])
                j = m * 8 + fc
                o.ts("dve", g.modT[:, j, :], ps[:, 0:2], bias[:, j:j + 1], None, ALU.add, None, [ps, bias], [g.modT])
        for (A, m, goff) in ((g.A1, 1, 0), (g.A2, 4, 8)):
            for fc in range(8):
                o.ts("dve", A[:, fc, :], g.modT[:, m * 8 + fc, :], 1.0, gn[:, goff + fc:goff + fc + 1],
                     ALU.add, ALU.mult, [g.modT, gn], [A])


def load_weight_bf16(g, ph, dst, src_ap, kchunks, ncols, stage, eng_cycle=("act", "dve")):
    o = g.o
    i = 0
    for kc in range(kchunks):
        c0 = 0
        while c0 < ncols:
            n = min(2048, ncols - c0)
            st = stage[i % len(stage)]
            o.dma(st[:, 0:n], src_ap[kc * 128:(kc + 1) * 128, c0:c0 + n], [], [st])
            o.copy(eng_cycle[i % len(eng_cycle)], dst[:, kc, c0:c0 + n], st[:, 0:n], [st], [dst])
            i += 1
            c0 += n


def norm_block(g, ph, xs, nb, A, sh_m, col, hT, sq, rstd, tmp):
    o = g.o
    o.act(sq[:, :, 0:nb], xs[:, :, 0:nb], AF.Square, [xs], [sq])
    ps = next_ps(g)
    for fc in range(8):
        o.mm(ps[:, 0:nb], g.ones_bf[:], sq[:, fc, 0:nb], fc == 0, fc == 7, [g.ones_bf, sq], [ps])
    o.ts("dve", rstd[:, 0:nb], ps[:, 0:nb], 1.0 / D, EPS, ALU.mult, ALU.add, [ps], [rstd])
    o.act(rstd[:, 0:nb], rstd[:, 0:nb], AF.Sqrt, [rstd], [rstd])
    o.recip(rstd[:, 0:nb], rstd[:, 0:nb], [rstd], [rstd])
    for fc in range(8):
        o.tt("dve", tmp[:, fc, 0:nb], xs[:, fc, 0:nb], rstd[:, 0:nb], ALU.mult, [xs, rstd], [tmp])
    for fc in range(8):
        if sh_m is None:
            o.act(hT[:, fc, 0:nb], tmp[:, fc, 0:nb], AF.Identity, [tmp, A], [hT], bias=0.0, scale=A[:, fc, col:col + 1])
        else:
            o.act(hT[:, fc, 0:nb], tmp[:, fc, 0:nb], AF.Identity, [tmp, A, g.modT], [hT],
                  bias=g.modT[:, sh_m * 8 + fc, col:col + 1], scale=A[:, fc, col:col + 1])


KMAJ = [
    (0, 128, [(0, 128, "U32T", 0)]), (128, 128, [(0, 128, "U32T", 128)]),
    (256, 128, [(0, 128, "MLQKT", 0)]), (384, 128, [(0, 128, "MLQKT", 128)]),
    (512, 128, [(0, 128, "MLQKT", 256)]), (640, 128, [(0, 128, "MLQKT", 384)]),
    (2064, 128, [(0, 128, "GLQT", 0)]), (2192, 128, [(0, 128, "GLKT", 0)]), (2832, 32, [(0, 32, "GLAT", 0)]),
]
TMAJ = [
    (0, 256, [(0, 256, "UTOK")]),
    (768, 512, [(0, 256, "MLV"), (256, 256, "MLO")]),
    (1280, 272, [(0, 16, "MLG"), (16, 256, "DAQ")]),
    (1552, 512, [(0, 256, "DAK"), (256, 256, "DAV")]),
    (2192, 384, [(0, 128, "GLK"), (128, 256, "GLV")]),
    (2576, 256, [(0, 256, "GLR")]),
]


def phase_p1(g, l, xsrc):
    o = g.o
    with Phase(g) as ph:
        stage = [ph.sb([128, 2048], F32, "wst") for _ in range(2)]
        win = ph.sb([128, 8, IN_DIM], BF16, "win")
        load_weight_bf16(g, ph, win, g.p["w_in"][l], 8, IN_DIM, stage)
        xs = [ph.sb([128, 8, 512], F32, "xs") for _ in range(2)]
        sq = ph.sb([128, 8, 512], BF16, "sq")
        tmp = ph.sb([128, 8, 512], F32, "tmp")
        rstd = ph.sb([128, 512], F32, "rstd")
        hT = ph.sb([128, 8, 512], BF16, "hT")
        kst = {}
        tst = {}
        evn = [0]

        def evac(out, in_, r, w):
            e = ("act", "dve")[evn[0] % 2]
            evn[0] += 1
            o.copy(e, out, in_, r, w)

        for bi, (t0, nb, isctx) in enumerate(blocks_of(g.Ttok)):
            col = 1 if isctx else 0
            x = xs[bi % 2]
            o.dma(x[:, :, 0:nb], xsrc[:, t0:t0 + nb].rearrange("(fc p) t -> p fc t", p=128), [], [x])
            import os
            kp1 = os.environ.get("KP1", "")
            if kp1 == "load":
                continue
            norm_block(g, ph, x, nb, g.A1, 0, col, hT, sq, rstd, tmp)
            if kp1 == "norm":
                continue
            for gi, (c0, ncol, subs) in enumerate(KMAJ):
                ps = next_ps(g)
                for fc in range(8):
                    o.mm(ps[0:ncol, 0:nb], win[:, fc, c0:c0 + ncol], hT[:, fc, 0:nb], fc == 0, fc == 7, [win, hT], [ps])
                for (s0, n, dname, r0) in subs:
                    dram = getattr(g, dname)
                    key = (gi, s0, bi % 2)
                    if key not in kst:
                        kst[key] = ph.sb([128, 512], dram.dtype, "kst")
                    st = kst[key]
                    evac(st[0:n, 0:nb], ps[s0:s0 + n, 0:nb], [ps], [st])
                    o.dma(dram[r0:r0 + n, t0:t0 + nb], st[0:n, 0:nb], [st], [])
            for ti in range(nb // 128 if kp1 != "kmaj" else 0):
                for gi, (c0, ncol, subs) in enumerate(TMAJ):
                    if os.environ.get("KTG") and str(gi) not in os.environ["KTG"]:
                        continue
                    ps = next_ps(g)
                    if os.environ.get("KC0"):
                        c0 = int(os.environ["KC0"])
                    for fc in range(8):
                        o.mm(ps[:, 0:ncol], hT[:, fc, ti * 128:(ti + 1) * 128], win[:, fc, c0:c0 + ncol],
                             fc == 0, fc == 7, [win, hT], [ps])
                    for (s0, n, dname) in subs:
                        dram = getattr(g, dname)
                        key = (gi, s0, ti % 2)
                        if key not in tst:
                            tst[key] = ph.sb([128, n], dram.dtype, "tst")
                        st = tst[key]
                        evac(st[:, 0:n], ps[:, s0:s0 + n], [ps], [st])
                        o.dma(dram[t0 + ti * 128:t0 + (ti + 1) * 128, :], st[:, 0:n], [st], [])


def phase_p3a(g, l, xsrc, last):
    o = g.o
    with Phase(g) as ph:
        stage = [ph.sb([128, 2048], F32, "wst") for _ in range(2)]
        wout = ph.sb([128, 8, D], BF16, "wout")
        load_weight_bf16(g, ph, wout, g.p["w_out"][l], 8, D, stage)
        gluw = ph.sb([128, 2, 256], BF16, "gluw")
        load_weight_bf16(g, ph, gluw, g.p["s5_glu_w"][l], 2, 256, stage)
        dcol = ph.sb([128, 4], F32, "dcol")
        load_cols(g, ph, g.p["s5_d"][l], 256, dcol, 0)
        load_cols(g, ph, g.p["s5_glu_b"][l], 256, dcol, 2)
        xs = ph.sb([128, 8, 512], F32, "xs")
        x1 = ph.sb([128, 8, 512], F32, "x1")
        cat = ph.sb([128, 8, 512], BF16, "cat")
        y5 = ph.sb([128, 2, 512], F32, "y5")
        u5 = ph.sb([128, 2, 512], F32, "u5")
        t1 = ph.sb([128, 2, 512], F32, "t1")
        t2 = ph.sb([128, 2, 512], F32, "t2")
        gb = ph.sb([128, 2, 512], BF16, "gb")
        gate = ph.sb([128, 512], F32, "gate")
        sq = ph.sb([128, 8, 512], BF16, "sq")
        tmp = ph.sb([128, 8, 512], F32, "tmp")
        rstd = ph.sb([128, 512], F32, "rstd")
        hT = ph.sb([128, 8, 512], BF16, "hT")
        for bi, (t0, nb, isctx) in enumerate(blocks_of(g.Ttok)):
            if isctx and last:
                continue
            col = 1 if isctx else 0
            o.dma(xs[:, :, 0:nb], xsrc[:, t0:t0 + nb].rearrange("(fc p) t -> p fc t", p=128), [], [xs])
            o.dma(cat[:, 2:8, 0:nb], g.CAT[256:1024, t0:t0 + nb].rearrange("(fc p) t -> p fc t", p=128), [], [cat])
            o.dma(y5[:, :, 0:nb], g.Y5T[:, t0:t0 + nb].rearrange("(fc p) t -> p fc t", p=128), [], [y5])
            o.dma(u5[:, :, 0:nb], g.U32T[:, t0:t0 + nb].rearrange("(fc p) t -> p fc t", p=128), [], [u5])
            for kc in range(2):
                o.stt("dve", t1[:, kc, 0:nb], u5[:, kc, 0:nb], dcol[:, kc:kc + 1], y5[:, kc, 0:nb], ALU.mult, ALU.add,
                      [u5, y5, dcol], [t1])
            o.tt("dve", t2[:, :, 0:nb], t1[:, :, 0:nb], t1[:, :, 0:nb], ALU.mult, [t1], [t2])
            o.ts("dve", t2[:, :, 0:nb], t2[:, :, 0:nb], 0.044715, 1.0, ALU.mult, ALU.add, [t2], [t2])
            o.tt("dve", t2[:, :, 0:nb], t2[:, :, 0:nb], t1[:, :, 0:nb], ALU.mult, [t2, t1], [t2])
            o.act(t2[:, :, 0:nb], t2[:, :, 0:nb], AF.Tanh, [t2], [t2], scale=math.sqrt(2.0 / math.pi))
            o.ts("dve", t2[:, :, 0:nb], t2[:, :, 0:nb], 1.0, 0.5, ALU.add, ALU.mult, [t2], [t2])
            o.tt("dve", t1[:, :, 0:nb], t2[:, :, 0:nb], t1[:, :, 0:nb], ALU.mult, [t2, t1], [t1])
            o.copy("act", gb[:, :, 0:nb], t1[:, :, 0:nb], [t1], [gb])
            for oc in range(2):
                ps = next_ps(g)
                for kc in range(2):
                    o.mm(ps[:, 0:nb], gluw[:, kc, oc * 128:(oc + 1) * 128], gb[:, kc, 0:nb], kc == 0, kc == 1, [gluw, gb], [ps])
                o.act(gate[:, 0:nb], ps[:, 0:nb], AF.Sigmoid, [ps, dcol], [gate], bias=dcol[:, 2 + oc:3 + oc])
                o.tt("dve", cat[:, oc, 0:nb], t1[:, oc, 0:nb], gate[:, 0:nb], ALU.mult, [t1, gate], [cat])
            for oc in range(8):
                ps = next_ps(g)
                for kc in range(8):
                    o.mm(ps[:, 0:nb], wout[:, kc, oc * 128:(oc + 1) * 128], cat[:, kc, 0:nb], kc == 0, kc == 7, [wout, cat], [ps])
                o.stt("dve", x1[:, oc, 0:nb], ps[:, 0:nb], g.modT[:, 2 * 8 + oc, col:col + 1], xs[:, oc, 0:nb],
                      ALU.mult, ALU.add, [ps, g.modT, xs], [x1])
            o.dma(g.X1T[:, t0:t0 + nb].rearrange("(fc p) t -> p fc t", p=128), x1[:, :, 0:nb], [x1], [])
            norm_block(g, ph, x1, nb, g.A2, 3, col, hT, sq, rstd, tmp)
            o.dma(g.H2T[:, t0:t0 + nb].rearrange("(fc p) t -> p fc t", p=128), hT[:, :, 0:nb], [hT], [])


def phase_p3b(g, l, last):
    o = g.o
    with Phase(g) as ph:
        stage = [ph.sb([128, 2048], F32, "wst")]
        w1 = ph.sb([128, 8, DFF], BF16, "w1")
        w2 = ph.sb([128, 32, D], BF16, "w2")
        load_weight_bf16(g, ph, w1, g.p["mlp_w1"][l], 8, DFF, stage)
        load_weight_bf16(g, ph, w2, g.p["mlp_w2"][l], 32, D, stage)
        h2 = ph.sb([128, 8, 512], BF16, "h2")
        x1 = ph.sb([128, 8, 512], F32, "x1")
        x2 = x1
        h1 = ph.sb([128, 32, 512], BF16, "h1")
        rl = [ph.sb([128, 512], F32, "rl") for _ in range(2)]
        if last:
            fcol = ph.sb([128, 8, 1], F32, "fcol")
            fc2 = ph.sb([128, 8], F32, "fc2")
            load_cols(g, ph, g.fng, 1024, fc2, 0)
            o.copy("dve", fcol[:, :, 0], fc2[:, :], [fc2], [fcol])
            sq = h2
            rstd = ph.sb([128, 512], F32, "rstd")
        for bi, (t0, nb, isctx) in enumerate(blocks_of(g.Ttok)):
            if isctx and last:
                continue
            col = 1 if isctx else 0
            o.dma(h2[:, :, 0:nb], g.H2T[:, t0:t0 + nb].rearrange("(fc p) t -> p fc t", p=128), [], [h2])
            o.dma(x1[:, :, 0:nb], g.X1T[:, t0:t0 + nb].rearrange("(fc p) t -> p fc t", p=128), [], [x1])
            for ff in range(32):
                ps = next_ps(g)
                for kc in range(8):
                    o.mm(ps[:, 0:nb], w1[:, kc, ff * 128:(ff + 1) * 128], h2[:, kc, 0:nb], kc == 0, kc == 7, [w1, h2], [ps])
                rt = rl[ff % 2]
                o.act(rt[:, 0:nb], ps[:, 0:nb], AF.Relu, [ps], [rt])
                o.tt("dve" if ff % 2 == 0 else "pool", h1[:, ff, 0:nb], rt[:, 0:nb], rt[:, 0:nb], ALU.mult, [rt], [h1])
            for oc in range(8):
                ps = next_ps(g)
                for ff in range(32):
                    o.mm(ps[:, 0:nb], w2[:, ff, oc * 128:(oc + 1) * 128], h1[:, ff, 0:nb], ff == 0, ff == 31, [w2, h1], [ps])
                o.stt("dve", x2[:, oc, 0:nb], ps[:, 0:nb], g.modT[:, 5 * 8 + oc, col:col + 1], x1[:, oc, 0:nb],
                      ALU.mult, ALU.add, [ps, g.modT, x1], [x2])
            if not last:
                o.dma(g.XT[:, t0:t0 + nb].rearrange("(fc p) t -> p fc t", p=128), x2[:, :, 0:nb], [x2], [])
            else:
                o.act(sq[:, :, 0:nb], x2[:, :, 0:nb], AF.Square, [x2], [sq])
                ps = next_ps(g)
                for fc in range(8):
                    o.mm(ps[:, 0:nb], g.ones_bf[:], sq[:, fc, 0:nb], fc == 0, fc == 7, [g.ones_bf, sq], [ps])
                o.ts("dve", rstd[:, 0:nb], ps[:, 0:nb], 1.0 / D, EPS, ALU.mult, ALU.add, [ps], [rstd])
                o.act(rstd[:, 0:nb], rstd[:, 0:nb], AF.Sqrt, [rstd], [rstd])
                o.recip(rstd[:, 0:nb], rstd[:, 0:nb], [rstd], [rstd])
                for fc in range(8):
                    o.stt("dve", x1[:, fc, 0:nb], x2[:, fc, 0:nb], fcol[:, fc, :], rstd[:, 0:nb],
                          ALU.mult, ALU.mult, [x2, fcol, rstd], [x1])
                o.dma(g.yT[:, t0 - CTX:t0 - CTX + nb].rearrange("(fc p) t -> p fc t", p=128), x1[:, :, 0:nb], [x1], [])


MIXERS = {"s5": True, "ml": True, "da": True, "gla": True}


def zero_dram_rows(g, ph, dram, r0, r1, dt):
    o = g.o
    z = ph.sb([128, 2048], dt, "zero")
    o.memset("dve", z[:], 0.0, [z])
    for rr in range(r0, r1, 128):
        for t0 in range(0, g.Ttok, 2048):
            n = min(2048, g.Ttok - t0)
            o.dma(dram[rr:rr + 128, t0:t0 + n], z[:, 0:n], [z], [])


def phase_mixers(g, l):
    with Phase(g) as ph:
        if not MIXERS["s5"]:
            zero_dram_rows(g, ph, g.Y5T, 0, 256, F32)
        if not MIXERS["ml"]:
            zero_dram_rows(g, ph, g.CAT, 256, 512, BF16)
        if not MIXERS["da"]:
            zero_dram_rows(g, ph, g.CAT, 512, 768, BF16)
        if not MIXERS["gla"]:
            zero_dram_rows(g, ph, g.CAT, 768, 1024, BF16)
    if MIXERS["s5"]:
        mixer_s5(g, l)
    if MIXERS["ml"]:
        mixer_lin(g, l, "ml")
    if MIXERS["gla"]:
        mixer_lin(g, l, "gla")
    if MIXERS["da"]:
        mixer_da(g, l)


def mixer_s5(g, l):
    o = g.o
    Ttok = g.Ttok
    nch = Ttok // 128
    PI = math.pi
    with Phase(g) as ph:
        idb = ph.sb([128, 128], BF16, "idb")
        jrb = ph.sb([128, 128], BF16, "jrb")
        o.copy("dve", idb[:], g.ident[:], [g.ident], [idb])
        o.copy("dve", jrb[:], g.jrev[:], [g.jrev], [jrb])
        perm_b = [idb, jrb]
        perm_f = [g.ident, g.jrev]
        iota = ph.sb([128, 512], F32, "iota")
        o.dma(iota[:], g.c["c_iota"][:, :], [], [iota])
        utok = ph.sb([128, nch, 16], BF16, "utok")
        Y = ph.sb([16, Ttok], F32, "Y")
        col = ph.sb([128, 16], F32, "col")
        MAG = ph.sb([128, 512], F32, "MAG")
        CT = ph.sb([128, 512], F32, "CT")
        TsA = ph.sb([128, 512], F32, "TsA")
        TsB = ph.sb([128, 512], F32, "TsB")
        ANG = ph.sb([128, 512], F32, "ANG")
        Bp = ph.sb([128, 16], F32, "Bp")
        Bps = ph.sb([128, 16], F32, "Bps")
        BB = ph.sb([128, 16], F32, "BB")
        Bcat = ph.sb([16, 128], BF16, "Bcat")
        Bcs = ph.sb([16, 128], BF16, "Bcs")
        Crow = ph.sb([16, 128], F32, "Crow")
        Ccat = ph.sb([128, 16], BF16, "Ccat")
        uT = ph.sb([16, 512], BF16, "uT")
        t1 = ph.sb([128, 512], F32, "t1")
        t2 = ph.sb([128, 512], F32, "t2")
        X = ph.sb([128, 512], F32, "X")
        Xs = ph.sb([128, 512], F32, "Xs")
        G1 = ph.sb([128, 512], F32, "G1")
        G2 = ph.sb([128, 512], F32, "G2")
        Hb = ph.sb([128, 512], BF16, "Hb")
        hp = ph.sb([128, 2], F32, "hp")
        tc_ = ph.sb([128, 4], F32, "tc")
        ytok = [ph.sb([128, 16], BF16, "ytok") for _ in range(2)]
        for gq in range(16):
            o.dma(utok[:], g.UTOK[:, 16 * gq:16 * gq + 16].rearrange("(c p) f -> p c f", p=128), [], [utok])
            for d in range(2):
                for half in range(2):
                    o.dma(col[64 * half:64 * half + 64, 0:1], g.p["s5_lam_re"][l][d][gq].rearrange("(p j) -> p j", j=1), [], [col])
                    o.dma(col[64 * half:64 * half + 64, 1:2], g.p["s5_lam_im"][l][d][gq].rearrange("(p j) -> p j", j=1), [], [col])
                o.dma(col[:, 2:3], g.p["s5_log_step"][l][d][gq:gq + 1].partition_broadcast(128), [], [col])
                o.dma(Bp[0:64, :], g.p["s5_b_re"][l][d][gq], [], [Bp])
                o.dma(Bp[64:128, :], g.p["s5_b_im"][l][d][gq], [], [Bp])
                o.dma(Bps[0:64, :], g.p["s5_b_im"][l][d][gq], [], [Bps])
                o.dma(Bps[64:128, :], g.p["s5_b_re"][l][d][gq], [], [Bps])
                o.dma(Crow[:, 0:64], g.p["s5_c_re"][l][d][gq], [], [Crow])
                o.dma(Crow[:, 64:128], g.p["s5_c_im"][l][d][gq], [], [Crow])
                o.act(col[:, 2:3], col[:, 2:3], AF.Exp, [col], [col])
                o.tt("dve", col[:, 3:4], col[:, 0:1], col[:, 2:3], ALU.mult, [col], [col])
                o.tt("dve", col[:, 4:5], col[:, 1:2], col[:, 2:3], ALU.mult, [col], [col])
                o.act(col[:, 5:6], col[:, 3:4], AF.Exp, [col], [col])
                o.ts("dve", MAG[:], g.ones_f[:, 0:1].to_broadcast([128, 512]) if False else iota[:], 0.0, col[:, 5:6], ALU.mult, ALU.add, [iota, col], [MAG])
                o.ts("dve", ANG[:], iota[:], col[:, 4:5], None, ALU.mult, None, [iota, col], [ANG])
                MAGIC = 12582912.0
                for (dstT, shift) in ((TsA, 0.0), (CT, 0.5 * PI)):
                    if shift != 0.0:
                        o.ts("dve", ANG[:], ANG[:], shift, None, ALU.add, None, [ANG], [ANG])
                    o.ts("dve", t1[:], ANG[:], 1.0 / (2 * PI), MAGIC, ALU.mult, ALU.add, [ANG], [t1])
                    o.ts("dve", t1[:], t1[:], -MAGIC, -2 * PI, ALU.add, ALU.mult, [t1], [t1])
                    o.tt("dve", t1[:], t1[:], ANG[:], ALU.add, [t1, ANG], [t1])
                    o.ts("dve", t1[:], t1[:], -PI, PI, ALU.max, ALU.min, [t1], [t1])
                    o.act(dstT[:], t1[:], AF.Sin, [t1], [dstT])
                o.ts("dve", TsA[64:128, :], TsA[64:128, :], -1.0, None, ALU.mult, None, [TsA], [TsA])
                o.ts("dve", TsB[:], TsA[:], -1.0, None, ALU.mult, None, [TsA], [TsB])
                o.tt("dve", col[:, 6:7], MAG[:, 0:1], CT[:, 0:1], ALU.mult, [MAG, CT], [col])
                o.tt("dve", col[:, 7:8], MAG[:, 0:1], TsB[:, 0:1], ALU.mult, [MAG, TsB], [col])
                o.ts("dve", col[0:64, 7:8], col[0:64, 7:8], -1.0, None, ALU.mult, None, [col], [col])
                o.ts("dve", col[:, 6:7], col[:, 6:7], -1.0, None, ALU.add, None, [col], [col])
                o.tt("dve", col[:, 8:9], col[:, 0:1], col[:, 0:1], ALU.mult, [col], [col])
                o.tt("dve", col[:, 9:10], col[:, 1:2], col[:, 1:2], ALU.mult, [col], [col])
                o.tt("dve", col[:, 8:9], col[:, 8:9], col[:, 9:10], ALU.add, [col], [col])
                o.recip(col[:, 8:9], col[:, 8:9], [col], [col])
                o.tt("dve", col[:, 9:10], col[:, 6:7], col[:, 0:1], ALU.mult, [col], [col])
                o.tt("dve", col[:, 10:11], col[:, 7:8], col[:, 1:2], ALU.mult, [col], [col])
                o.tt("dve", col[:, 9:10], col[:, 9:10], col[:, 10:11], ALU.add, [col], [col])
                o.tt("dve", col[:, 9:10], col[:, 9:10], col[:, 8:9], ALU.mult, [col], [col])
                o.tt("dve", col[:, 10:11], col[:, 7:8], col[:, 0:1], ALU.mult, [col], [col])
                o.tt("dve", col[:, 11:12], col[:, 6:7], col[:, 1:2], ALU.mult, [col], [col])
                o.tt("dve", col[:, 10:11], col[:, 10:11], col[:, 11:12], ALU.subtract, [col], [col])
                o.tt("dve", col[:, 10:11], col[:, 10:11], col[:, 8:9], ALU.mult, [col], [col])
                o.ts("dve", col[0:64, 10:11], col[0:64, 10:11], -1.0, None, ALU.mult, None, [col], [col])
                o.ts("dve", BB[:], Bp[:], col[:, 9:10], None, ALU.mult, None, [Bp, col], [BB])
                o.stt("dve", BB[:], Bps[:], col[:, 10:11], BB[:], ALU.mult, ALU.add, [Bps, col, BB], [BB])
                ps = next_ps(g)
                o.tr(ps[0:16, 0:128], BB[:, :], g.ident[:, :], [BB, g.ident], [ps])
                o.copy("dve", Bcat[:], ps[0:16, 0:128], [ps], [Bcat])
                o.copy("dve", Bcs[:, 0:64], ps[0:16, 64:128], [ps], [Bcs])
                o.copy("dve", Bcs[:, 64:128], ps[0:16, 0:64], [ps], [Bcs])
                o.ts("dve", Crow[:, 64:128], Crow[:, 64:128], -1.0, None, ALU.mult, None, [Crow], [Crow])
                ps = next_ps(g)
                o.tr(ps[:, 0:16], Crow[:, :], g.ident[0:16, 0:16], [Crow, g.ident], [ps])
                o.copy("dve", Ccat[:], ps[:, 0:16], [ps], [Ccat])
                o.memset("dve", hp[:], 0.0, [hp])
                if d == 0:
                    chunks = [[0, 1]] + [list(range(c, min(c + 4, nch))) for c in range(2, nch, 4)]
                else:
                    chunks = [[1, 0]] + [list(range(c, max(c - 4, 1), -1)) for c in range(nch - 1, 1, -4)]
                yi = 0
                for cl in chunks:
                    nb = 128 * len(cl)
                    ups = next_ps(g)
                    for i, c in enumerate(cl):
                        o.mm(ups[0:16, i * 128:(i + 1) * 128], utok[:, c, :], perm_b[d][:], True, True, [utok, perm_b[d]], [ups])
                    o.copy("act", uT[:, 0:nb], ups[0:16, 0:nb], [ups], [uT])
                    bu = next_ps(g)
                    o.mm(bu[:, 0:nb], Bcat[:], uT[:, 0:nb], True, True, [Bcat, uT], [bu])
                    bus = next_ps(g)
                    o.mm(bus[:, 0:nb], Bcs[:], uT[:, 0:nb], True, True, [Bcs, uT], [bus])
                    o.tt("dve", t1[:, 0:nb], bu[:, 0:nb], CT[:, 0:nb], ALU.mult, [bu, CT], [t1])
                    o.tt("dve", t2[:, 0:nb], bus[:, 0:nb], TsA[:, 0:nb], ALU.mult, [bus, TsA], [t2])
                    o.tt("pool", X[:, 0:nb], t1[:, 0:nb], t2[:, 0:nb], ALU.add, [t1, t2], [X])
                    o.tt("dve", t1[:, 0:nb], bus[:, 0:nb], CT[:, 0:nb], ALU.mult, [bus, CT], [t1])
                    o.tt("dve", t2[:, 0:nb], bu[:, 0:nb], TsB[:, 0:nb], ALU.mult, [bu, TsB], [t2])
                    o.tt("pool", Xs[:, 0:nb], t1[:, 0:nb], t2[:, 0:nb], ALU.add, [t1, t2], [Xs])
                    o.scan(G1[:, 0:nb], MAG[:, 0:nb], X[:, 0:nb], hp[:, 0:1], ALU.mult, ALU.add, [MAG, X, hp], [G1])
                    o.scan(G2[:, 0:nb], MAG[:, 0:nb], Xs[:, 0:nb], hp[:, 1:2], ALU.mult, ALU.add, [MAG, Xs, hp], [G2])
                    o.tt("dve", t1[:, 0:nb], G1[:, 0:nb], CT[:, 0:nb], ALU.mult, [G1, CT], [t1])
                    o.tt("pool", t2[:, 0:nb], G2[:, 0:nb], TsB[:, 0:nb], ALU.mult, [G2, TsB], [t2])
                    o.tt("dve", Hb[:, 0:nb], t1[:, 0:nb], t2[:, 0:nb], ALU.add, [t1, t2], [Hb])
                    o.tt("dve", hp[:, 0:1], t1[:, nb - 1:nb], t2[:, nb - 1:nb], ALU.add, [t1, t2], [hp])
                    o.tt("dve", tc_[:, 0:1], G2[:, nb - 1:nb], CT[:, nb - 1:nb], ALU.mult, [G2, CT], [tc_])
                    o.tt("dve", tc_[:, 1:2], G1[:, nb - 1:nb], TsA[:, nb - 1:nb], ALU.mult, [G1, TsA], [tc_])
                    o.tt("dve", hp[:, 1:2], tc_[:, 0:1], tc_[:, 1:2], ALU.add, [tc_], [hp])
                    for i, c in enumerate(cl):
                        yp = next_ps(g)
                        o.mm(yp[:, 0:16], Hb[:, i * 128:(i + 1) * 128], Ccat[:], True, True, [Hb, Ccat], [yp])
                        yt = ytok[yi % 2]
                        yi += 1
                        o.copy("act", yt[:], yp[:, 0:16], [yp], [yt])
                        yq = next_ps(g)
                        o.mm(yq[0:16, 0:128], yt[:], perm_b[d][:], True, True, [yt, perm_b[d]], [yq])
                        if d == 0:
                            o.copy("act", Y[:, c * 128:(c + 1) * 128], yq[0:16, 0:128], [yq], [Y])
                        else:
                            o.tt("dve", Y[:, c * 128:(c + 1) * 128], Y[:, c * 128:(c + 1) * 128], yq[0:16, 0:128], ALU.add, [Y, yq], [Y])
            o.dma(g.Y5T[16 * gq:16 * gq + 16, :], Y[:], [Y], [])


def mixer_ml(g, l):
    o = g.o
    Ttok = g.Ttok
    nch = Ttok // 128
    LN8 = math.log(1.0 / 8.0)
    for h in range(4):
        with Phase(g) as ph:
            triS = [ph.sb([128, 128], F32, "triS") for _ in range(2)]
            sufS = [ph.sb([128, 128], F32, "sufS") for _ in range(2)]
            mask = [g.triu, g.tril]
            for d in range(2):
                o.ts("dve", triS[d][:], mask[d][:], -1.0, None, ALU.mult, None, [mask[d]], [triS[d]])
                o.ts("dve", sufS[d][:], mask[d][:], 1.0, -1.0, ALU.mult, ALU.add, [mask[d]], [sufS[d]])
            qT = ph.sb([64, Ttok], BF16, "qT")
            kT = ph.sb([64, Ttok], BF16, "kT")
            ktok = ph.sb([128, nch, 64], BF16, "ktok")
            v1 = ph.sb([128, nch, 65], BF16, "v1")
            O = ph.sb([128, nch, 64], F32, "O")
            o.dma(v1[:, :, 0:64], g.MLV[:, 64 * h:64 * h + 64].rearrange("(c p) f -> p c f", p=128), [], [v1])
            o.memset("pool", v1[:, :, 64:65], 1.0, [v1])
            SCT = 2048
            xin = ph.sb([64, SCT + 2], F32, "xin")
            cv = ph.sb([64, SCT], F32, "cv")
            wrow = ph.sb([4, 64], F32, "wrow")
            wcol = ph.sb([64, 4], F32, "wcol")
            for qi, (dst, r0, c0w) in enumerate(((qT, 64 * h, 64 * h), (kT, 256 + 64 * h, 256 + 64 * h))):
                o.dma(wrow[0:3, :], g.p["ml_conv_w"][l][:, c0w:c0w + 64], [], [wrow])
                o.dma(wrow[3:4, :], g.p["ml_conv_b"][l][c0w:c0w + 64].rearrange("(j p) -> j p", j=1), [], [wrow])
                ps = next_ps(g)
                o.tr(ps[0:64, 0:4], wrow[0:4, 0:64], g.ident[0:4, 0:4], [wrow, g.ident], [ps])
                o.copy("dve", wcol[:], ps[0:64, 0:4], [ps], [wcol])
                segs = [(0, CTX)]
                t = CTX
                while t < Ttok:
                    n = min(SCT, Ttok - t)
                    segs.append((t, n))
                    t += n
                for (t0, n) in segs:
                    first = (t0 == 0 or t0 == CTX)
                    lastseg = (t0 + n == CTX or t0 + n == Ttok)
                    lo = 1 if first else 0
                    hi = n + 1 if lastseg else n + 2
                    if first:
                        o.memset("dve", xin[:, 0:1], 0.0, [xin])
                    if lastseg:
                        o.memset("dve", xin[:, n + 1:n + 2], 0.0, [xin])
                    o.dma(xin[:, lo:hi], g.MLQKT[r0:r0 + 64, t0 - 1 + lo:t0 - 1 + hi], [], [xin])
                    o.ts("dve", cv[:, 0:n], xin[:, 0:n], wcol[:, 0:1], None, ALU.mult, None, [xin, wcol], [cv])
                    o.stt("dve", cv[:, 0:n], xin[:, 1:n + 1], wcol[:, 1:2], cv[:, 0:n], ALU.mult, ALU.add, [xin, wcol, cv], [cv])
                    o.stt("dve", cv[:, 0:n], xin[:, 2:n + 2], wcol[:, 2:3], cv[:, 0:n], ALU.mult, ALU.add, [xin, wcol, cv], [cv])
                    o.act(cv[:, 0:n], cv[:, 0:n], AF.Silu, [cv, wcol], [cv], bias=wcol[:, 3:4])
                    o.copy("pool", dst[:, t0:t0 + n], cv[:, 0:n], [cv], [dst])
                    if qi == 1:
                        for ci in range(n // 128):
                            ps = next_ps(g)
                            o.tr(ps[:, 0:64], cv[:, ci * 128:(ci + 1) * 128], g.ident[0:64, 0:64], [cv, g.ident], [ps])
                            o.copy("act" if ci % 2 == 0 else "dve", ktok[:, t0 // 128 + ci, :], ps[:, 0:64], [ps], [ktok])
            gt = ph.sb([128, nch, 16], F32, "gt")
            gb = ph.sb([128, 16], F32, "gb")
            o.dma(gt[:], g.MLG.rearrange("(c p) f -> p c f", p=128), [], [gt])
            o.dma(gb[:], g.p["ml_gate_b"][l].partition_broadcast(128), [], [gb])
            igs = [ph.sb([128, nch], F32, "igs") for _ in range(2)]
            ig8 = [ph.sb([128, nch], F32, "ig8") for _ in range(2)]
            Lfs = [ph.sb([128, nch], F32, "Lfs") for _ in range(2)]
            for d in range(2):
                ci_ = (2 * d) * 4 + h
                cf_ = (2 * d + 1) * 4 + h
                o.ts("dve", igs[d][:], gt[:, :, ci_], gb[:, ci_:ci_ + 1], None, ALU.add, None, [gt, gb], [igs[d]])
                o.ts("dve", ig8[d][:], igs[d][:], LN8, None, ALU.add, None, [igs[d]], [ig8[d]])
                o.ts("dve", Lfs[d][:], gt[:, :, cf_], gb[:, cf_:cf_ + 1], None, ALU.add, None, [gt, gb], [Lfs[d]])
                o.act(Lfs[d][:], Lfs[d][:], AF.Exp, [Lfs[d]], [Lfs[d]], scale=-1.0)
                o.act(Lfs[d][:], Lfs[d][:], AF.Ln, [Lfs[d]], [Lfs[d]], bias=1.0)
            S = ph.sb([64, 65], F32, "S")
            Sb = ph.sb([64, 65], BF16, "Sb")
            Lt = [ph.sb([128, 64], F32, "Lt") for _ in range(2)]
            IG = [ph.sb([128, 64], F32, "IG") for _ in range(2)]
            eb = [ph.sb([64, 128], F32, "eb") for _ in range(2)]
            enb = [ph.sb([64, 128], F32, "enb") for _ in range(2)]
            ebt = [ph.sb([64, 1], F32, "ebt") for _ in range(2)]
            qt = [ph.sb([64, 128], BF16, "qt") for _ in range(2)]
            kt = [ph.sb([64, 128], BF16, "kt") for _ in range(2)]
            esuf = [ph.sb([128, 64], F32, "esuf") for _ in range(2)]
            khat = [ph.sb([128, 64], BF16, "khat") for _ in range(2)]
            PT = [ph.sb([128, 128], BF16, "PT") for _ in range(2)]
            rd = [ph.sb([128, 1], F32, "rd") for _ in range(2)]
            it = 0
            for d in range(2):
                o.memset("dve", S[:], 0.0, [S])
                o.memset("pool", Sb[:], 0.0, [Sb])
                order = list(range(nch)) if d == 0 else [1, 0] + list(range(nch - 1, 1, -1))
                tl = 127 if d == 0 else 0
                for c in order:
                    j = it % 2
                    it += 1
                    cs = slice(c * 128, (c + 1) * 128)
                    o.ts("dve", Lt[j][:], g.ones_f[:, 0:64], Lfs[d][:, c:c + 1], None, ALU.mult, None, [g.ones_f, Lfs[d]], [Lt[j]])
                    o.ts("dve", IG[j][:], g.ones_f[:, 0:64], igs[d][:, c:c + 1], None, ALU.mult, None, [g.ones_f, igs[d]], [IG[j]])
                    bps = next_ps(g)
                    o.mm(bps[0:64, 0:128], Lt[j][:], triS[d][:], True, True, [Lt[j], triS[d]], [bps])
                    nps = next_ps(g)
                    o.mm(nps[0:64, 0:128], Lt[j][:], mask[d][:], True, False, [Lt[j], mask[d]], [nps])
                    o.mm(nps[0:64, 0:128], IG[j][:], g.ident[:, :], False, True, [IG[j], g.ident], [nps])
                    sps = next_ps(g)
                    o.mm(sps[:, 0:64], sufS[d][:], Lt[j][:], True, True, [Lt[j], sufS[d]], [sps])
                    o.act(eb[j][:], bps[0:64, 0:128], AF.Exp, [bps], [eb[j]])
                    o.act(enb[j][:], nps[0:64, 0:128], AF.Exp, [nps], [enb[j]], bias=LN8)
                    o.act(ebt[j][:], bps[0:64, tl:tl + 1], AF.Exp, [bps], [ebt[j]])
                    o.act(esuf[j][:], sps[:, 0:64], AF.Exp, [sps, ig8[d]], [esuf[j]], bias=ig8[d][:, c:c + 1])
                    o.tt("dve", qt[j][:], qT[:, cs], eb[j][:], ALU.mult, [qT, eb[j]], [qt[j]])
                    o.tt("pool", kt[j][:], kT[:, cs], enb[j][:], ALU.mult, [kT, enb[j]], [kt[j]])
                    o.tt("pool", khat[j][:], ktok[:, c, :], esuf[j][:], ALU.mult, [ktok, esuf[j]], [khat[j]])
                    aps = next_ps(g)
                    o.mm(aps[:, 0:128], kt[j][:], qt[j][:], True, True, [kt[j], qt[j]], [aps])
                    o.tt("dve", PT[j][:], aps[:, 0:128], mask[d][:], ALU.mult, [aps, mask[d]], [PT[j]])
                    ops_ = next_ps(g)
                    o.mm(ops_[:, 0:65], PT[j][:], v1[:, c, :], True, False, [PT[j], v1], [ops_])
                    o.mm(ops_[:, 0:65], qt[j][:], Sb[:], False, True, [qt[j], Sb], [ops_])
                    o.act(rd[j][:], ops_[:, 64:65], AF.Abs, [ops_], [rd[j]])
                    o.ts("dve", rd[j][:], rd[j][:], 1.0, None, ALU.max, None, [rd[j]], [rd[j]])
                    o.recip(rd[j][:], rd[j][:], [rd[j]], [rd[j]])
                    if d == 0:
                        o.ts("dve", O[:, c, :], ops_[:, 0:64], rd[j][:, 0:1], None, ALU.mult, None, [ops_, rd[j]], [O])
                    else:
                        o.stt("dve", O[:, c, :], ops_[:, 0:64], rd[j][:, 0:1], O[:, c, :], ALU.mult, ALU.add, [ops_, rd[j], O], [O])
                    ups = next_ps(g)
                    o.mm(ups[0:64, 0:65], khat[j][:], v1[:, c, :], True, True, [khat[j], v1], [ups])
                    o.stt("dve", S[:], S[:], ebt[j][:, 0:1], ups[0:64, 0:65], ALU.mult, ALU.add, [S, ebt[j], ups], [S])
                    o.copy("act", Sb[:], S[:], [S], [Sb])
            gain = ph.sb([128, 64], F32, "gain")
            o.dma(gain[:], g.p["ml_norm_g"][l][64 * h:64 * h + 64].partition_broadcast(128), [], [gain])
            SC = 8
            rr = ph.sb([128, SC, 64], F32, "rr")
            sq = ph.sb([128, SC, 64], F32, "sq")
            ss = ph.sb([128, SC], F32, "ss")
            yb = ph.sb([64, SC * 128], BF16, "yb")
            for c0 in range(0, nch, SC):
                n = min(SC, nch - c0)
                o.dma(rr[:, 0:n, :], g.MLO[c0 * 128:(c0 + n) * 128, 64 * h:64 * h + 64].rearrange("(c p) f -> p c f", p=128), [], [rr])
                o.act(rr[:, 0:n, :], rr[:, 0:n, :], AF.Sigmoid, [rr], [rr])
                o.tt("dve", sq[:, 0:n, :], O[:, c0:c0 + n, :], O[:, c0:c0 + n, :], ALU.mult, [O], [sq])
                o.red(ss[:, 0:n], sq[:, 0:n, :], ALU.add, [sq], [ss])
                o.ts("dve", ss[:, 0:n], ss[:, 0:n], 1.0 / 64, EPS, ALU.mult, ALU.add, [ss], [ss])
                o.act(ss[:, 0:n], ss[:, 0:n], AF.Sqrt, [ss], [ss])
                o.recip(ss[:, 0:n], ss[:, 0:n], [ss], [ss])
                for ci in range(n):
                    o.stt("dve", sq[:, ci, :], O[:, c0 + ci, :], ss[:, ci:ci + 1], gain[:], ALU.mult, ALU.mult, [O, ss, gain], [sq])
                o.tt("dve", sq[:, 0:n, :], sq[:, 0:n, :], rr[:, 0:n, :], ALU.mult, [sq, rr], [sq])
                for ci in range(n):
                    ps = next_ps(g)
                    o.tr(ps[0:64, 0:128], sq[:, ci, :], g.ident[:, :], [sq, g.ident], [ps])
                    o.copy("act" if ci % 2 == 0 else "dve", yb[:, ci * 128:(ci + 1) * 128], ps[0:64, 0:128], [ps], [yb])
                o.dma(g.CAT[256 + 64 * h:256 + 64 * h + 64, c0 * 128:(c0 + n) * 128], yb[:, 0:n * 128], [yb], [])


def mixer_lin(g, l, kind):
    if kind != "gla":
        return mixer_ml(g, l)
    o = g.o
    Ttok = g.Ttok
    nch = Ttok // 128
    with Phase(g) as ph0:
        pass
    for h in range(4):
        with Phase(g) as ph:
            triS = [ph.sb([128, 128], F32, "triS") for _ in range(2)]
            sufS = [ph.sb([128, 128], F32, "sufS") for _ in range(2)]
            mask = [g.triu, g.tril]
            for d in range(2):
                o.ts("dve", triS[d][:], mask[d][:], -1.0 / 16, None, ALU.mult, None, [mask[d]], [triS[d]])
                o.ts("dve", sufS[d][:], mask[d][:], 1.0 / 16, -1.0 / 16, ALU.mult, ALU.add, [mask[d]], [sufS[d]])
            qT = ph.sb([32, Ttok], BF16, "qT")
            kT = ph.sb([32, Ttok], BF16, "kT")
            ktok = ph.sb([128, nch, 32], BF16, "ktok")
            v = ph.sb([128, nch, 64], BF16, "v")
            aT = ph.sb([16, Ttok], BF16, "aT")
            O = ph.sb([128, nch, 64], F32, "O")
            o.dma(qT[:], g.GLQT[32 * h:32 * h + 32, :], [], [qT])
            o.dma(kT[:], g.GLKT[32 * h:32 * h + 32, :], [], [kT])
            o.dma(ktok[:], g.GLK[:, 32 * h:32 * h + 32].rearrange("(c p) f -> p c f", p=128), [], [ktok])
            o.dma(v[:], g.GLV[:, 64 * h:64 * h + 64].rearrange("(c p) f -> p c f", p=128), [], [v])
            waf = ph.sb([16, 32], F32, "waf")
            wa = ph.sb([16, 32], BF16, "wa")
            baf = ph.sb([1, 32], F32, "baf")
            ba = ph.sb([1, 32], BF16, "ba")
            S = ph.sb([32, 64], F32, "S")
            Sb = ph.sb([32, 64], BF16, "Sb")
            e1 = [ph.sb([128, 32], F32, "e1") for _ in range(2)]
            Lt = [ph.sb([128, 32], F32, "Lt") for _ in range(2)]
            eb = [ph.sb([32, 128], F32, "eb") for _ in range(2)]
            enb = [ph.sb([32, 128], F32, "enb") for _ in range(2)]
            ebt = [ph.sb([32, 1], F32, "ebt") for _ in range(2)]
            qt = [ph.sb([32, 128], BF16, "qt") for _ in range(2)]
            kt = [ph.sb([32, 128], BF16, "kt") for _ in range(2)]
            esuf = [ph.sb([128, 32], F32, "esuf") for _ in range(2)]
            khat = [ph.sb([128, 32], BF16, "khat") for _ in range(2)]
            PT = [ph.sb([128, 128], BF16, "PT") for _ in range(2)]
            it = 0
            for d in range(2):
                o.dma(aT[:], g.GLAT[16 * d:16 * d + 16, :], [], [aT])
                o.dma(waf[:], g.p["gla_alpha_w"][l][d][:, 32 * h:32 * h + 32], [], [waf])
                o.copy("dve", wa[:], waf[:], [waf], [wa])
                o.dma(baf[:], g.p["gla_alpha_b"][l][d][32 * h:32 * h + 32].rearrange("(j p) -> j p", j=1), [], [baf])
                o.copy("dve", ba[:], baf[:], [baf], [ba])
                o.memset("dve", S[:], 0.0, [S])
                o.memset("pool", Sb[:], 0.0, [Sb])
                order = list(range(nch)) if d == 0 else [1, 0] + list(range(nch - 1, 1, -1))
                tl = 127 if d == 0 else 0
                for c in order:
                    j = it % 2
                    it += 1
                    cs = slice(c * 128, (c + 1) * 128)
                    xps = next_ps(g)
                    o.mm(xps[:, 0:32], aT[0:16, cs], wa[:], True, False, [aT, wa], [xps])
                    o.mm(xps[:, 0:32], g.ones_bf[0:1, 0:128], ba[:], False, True, [g.ones_bf, ba], [xps])
                    o.act(e1[j][:], xps[:, 0:32], AF.Exp, [xps], [e1[j]], scale=-1.0)
                    o.act(Lt[j][:], e1[j][:], AF.Ln, [e1[j]], [Lt[j]], bias=1.0)
                    bps = next_ps(g)
                    o.mm(bps[0:32, 0:128], Lt[j][:], triS[d][:], True, True, [Lt[j], triS[d]], [bps])
                    sps = next_ps(g)
                    o.mm(sps[:, 0:32], sufS[d][:], Lt[j][:], True, True, [Lt[j], sufS[d]], [sps])
                    o.act(eb[j][:], bps[0:32, 0:128], AF.Exp, [bps], [eb[j]], bias=math.log(32 ** -0.5))
                    o.act(enb[j][:], bps[0:32, 0:128], AF.Exp, [bps], [enb[j]], scale=-1.0)
                    o.act(ebt[j][:], bps[0:32, tl:tl + 1], AF.Exp, [bps], [ebt[j]])
                    o.act(esuf[j][:], sps[:, 0:32], AF.Exp, [sps], [esuf[j]])
                    o.tt("dve", qt[j][:], qT[:, cs], eb[j][:], ALU.mult, [qT, eb[j]], [qt[j]])
                    o.tt("pool", kt[j][:], kT[:, cs], enb[j][:], ALU.mult, [kT, enb[j]], [kt[j]])
                    o.tt("pool", khat[j][:], ktok[:, c, :], esuf[j][:], ALU.mult, [ktok, esuf[j]], [khat[j]])
                    aps = next_ps(g)
                    o.mm(aps[:, 0:128], kt[j][:], qt[j][:], True, True, [kt[j], qt[j]], [aps])
                    o.tt("dve", PT[j][:], aps[:, 0:128], mask[d][:], ALU.mult, [aps, mask[d]], [PT[j]])
                    ops_ = next_ps(g)
                    o.mm(ops_[:, 0:64], PT[j][:], v[:, c, :], True, False, [PT[j], v], [ops_])
                    o.mm(ops_[:, 0:64], qt[j][:], Sb[:], False, True, [qt[j], Sb], [ops_])
                    if d == 0:
                        o.copy("act", O[:, c, :], ops_[:, 0:64], [ops_], [O])
                    else:
                        o.tt("dve", O[:, c, :], O[:, c, :], ops_[:, 0:64], ALU.add, [O, ops_], [O])
                    ups = next_ps(g)
                    o.mm(ups[0:32, 0:64], khat[j][:], v[:, c, :], True, True, [khat[j], v], [ups])
                    o.stt("dve", S[:], S[:], ebt[j][:, 0:1], ups[0:32, 0:64], ALU.mult, ALU.add, [S, ebt[j], ups], [S])
                    o.copy("act", Sb[:], S[:], [S], [Sb])
            gain = ph.sb([128, 64], F32, "gain")
            o.dma(gain[:], g.p["gla_norm_g"][l].partition_broadcast(128), [], [gain])
            SC = 8
            rr = ph.sb([128, SC, 64], F32, "rr")
            sq = ph.sb([128, SC, 64], F32, "sq")
            ss = ph.sb([128, SC], F32, "ss")
            yb = ph.sb([64, SC * 128], BF16, "yb")
            for c0 in range(0, nch, SC):
                n = min(SC, nch - c0)
                o.dma(rr[:, 0:n, :], g.GLR[c0 * 128:(c0 + n) * 128, 64 * h:64 * h + 64].rearrange("(c p) f -> p c f", p=128), [], [rr])
                o.act(rr[:, 0:n, :], rr[:, 0:n, :], AF.Silu, [rr], [rr])
                o.tt("dve", sq[:, 0:n, :], O[:, c0:c0 + n, :], O[:, c0:c0 + n, :], ALU.mult, [O], [sq])
                o.red(ss[:, 0:n], sq[:, 0:n, :], ALU.add, [sq], [ss])
                o.ts("dve", ss[:, 0:n], ss[:, 0:n], 1.0 / 64, EPS, ALU.mult, ALU.add, [ss], [ss])
                o.act(ss[:, 0:n], ss[:, 0:n], AF.Sqrt, [ss], [ss])
                o.recip(ss[:, 0:n], ss[:, 0:n], [ss], [ss])
                for ci in range(n):
                    o.stt("dve", sq[:, ci, :], O[:, c0 + ci, :], ss[:, ci:ci + 1], gain[:], ALU.mult, ALU.mult, [O, ss, gain], [sq])
                o.tt("dve", sq[:, 0:n, :], sq[:, 0:n, :], rr[:, 0:n, :], ALU.mult, [sq, rr], [sq])
                for ci in range(n):
                    ps = next_ps(g)
                    o.tr(ps[0:64, 0:128], sq[:, ci, :], g.ident[:, :], [sq, g.ident], [ps])
                    o.copy("act" if ci % 2 == 0 else "dve", yb[:, ci * 128:(ci + 1) * 128], ps[0:64, 0:128], [ps], [yb])
                o.dma(g.CAT[768 + 64 * h:768 + 64 * h + 64, c0 * 128:(c0 + n) * 128], yb[:, 0:n * 128], [yb], [])


def load_cols_small(g, ph, vec_ap, n, dst, dcol=0):
    o = g.o
    rows = ph.sb([1, 128], F32, "lcs_rows")
    o.dma(rows[0:1, 0:n], vec_ap.rearrange("(j p) -> j p", j=1), [], [rows])
    ps = next_ps(g)
    o.tr(ps[0:n, 0:1], rows[0:1, 0:n], g.ident[0:1, 0:1], [rows, g.ident], [ps])
    o.copy("dve", dst[0:n, dcol:dcol + 1], ps[0:n, 0:1], [ps], [dst])


def mixer_da(g, l):
    o = g.o
    Ttok = g.Ttok
    nch = Ttok // 128
    scale = 32 ** -0.5
    lam_init = 0.8 - 0.6 * math.exp(-0.3 * l)
    SC = 16
    for h in range(4):
        with Phase(g) as ph:
            KT = [ph.sb([33, Ttok], BF16, "KT") for _ in range(2)]
            QT = [ph.sb([33, Ttok], BF16, "QT") for _ in range(2)]
            V1 = ph.sb([128, nch, 65], BF16, "V1")
            o.dma(V1[:, :, 0:64], g.DAV[:, 64 * h:64 * h + 64].rearrange("(c p) f -> p c f", p=128), [], [V1])
            o.memset("pool", V1[:, :, 64:65], 1.0, [V1])
            cos = ph.sb([128, SC, 16], F32, "cos")
            sin = ph.sb([128, SC, 16], F32, "sin")
            raw = ph.sb([128, SC, 64], F32, "raw")
            sq = ph.sb([128, SC, 64], F32, "sqr")
            nsq = ph.sb([128, SC, 2], F32, "nsq")
            rot = ph.sb([128, SC, 2, 33], F32, "rot")
            ta = ph.sb([128, SC, 16], F32, "ta")
            tb = ph.sb([128, SC, 16], F32, "tb")
            kmx = ph.sb([128, 2], F32, "kmx")
            kmt = ph.sb([128, 2], F32, "kmt")
            o.memset("dve", kmx[:], 0.0, [kmx])
            for c0 in range(0, nch, SC):
                n = min(SC, nch - c0)
                o.dma(raw[:, 0:n, :], g.DAK[c0 * 128:(c0 + n) * 128, 64 * h:64 * h + 64].rearrange("(c p) f -> p c f", p=128), [], [raw])
                o.tt("dve", sq[:, 0:n, :], raw[:, 0:n, :], raw[:, 0:n, :], ALU.mult, [raw], [sq])
                for m in range(2):
                    o.red(nsq[:, 0:n, m], sq[:, 0:n, 32 * m:32 * m + 32], ALU.add, [sq], [nsq])
                    o.red(kmt[:, m:m + 1], nsq[:, 0:n, m], ALU.max, [nsq], [kmt])
                o.tt("dve", kmx[:], kmx[:], kmt[:], ALU.max, [kmx, kmt], [kmx])
            ps = next_ps(g)
            o.tr(ps[0:2, 0:128], kmx[:, 0:2], g.ident[:, :], [kmx, g.ident], [ps])
            kcol = ph.sb([2, 1], F32, "kcol")
            o.red(kcol[0:2, 0:1], ps[0:2, 0:128], ALU.max, [ps], [kcol])
            dg = ph.sb([2, 2], F32, "dg")
            o.ts("dve", dg[:], g.ident[0:2, 0:2], kcol[0:2, 0:1], None, ALU.mult, None, [g.ident, kcol], [dg])
            ps = next_ps(g)
            o.mm(ps[:, 0:2], g.ones_f[0:2, 0:128], dg[:], True, True, [g.ones_f, dg], [ps])
            kbc = ph.sb([128, 2], F32, "kbc")
            o.copy("dve", kbc[:], ps[:, 0:2], [ps], [kbc])
            evn = 0
            for c0 in range(0, nch, SC):
                n = min(SC, nch - c0)
                o.dma(cos[:, 0:n, :], g.c["c_cos"][c0 * 128:(c0 + n) * 128, :].rearrange("(c p) f -> p c f", p=128), [], [cos])
                o.dma(sin[:, 0:n, :], g.c["c_sin"][c0 * 128:(c0 + n) * 128, :].rearrange("(c p) f -> p c f", p=128), [], [sin])
                for (src, dstT, isq) in ((g.DAQ, QT, True), (g.DAK, KT, False)):
                    o.dma(raw[:, 0:n, :], src[c0 * 128:(c0 + n) * 128, 64 * h:64 * h + 64].rearrange("(c p) f -> p c f", p=128), [], [raw])
                    if isq:
                        o.tt("dve", sq[:, 0:n, :], raw[:, 0:n, :], raw[:, 0:n, :], ALU.mult, [raw], [sq])
                    for m in range(2):
                        x1 = raw[:, 0:n, 32 * m:32 * m + 32:2]
                        x2 = raw[:, 0:n, 32 * m + 1:32 * m + 32:2]
                        o.tt("dve", ta[:, 0:n, :], x1, cos[:, 0:n, :], ALU.mult, [raw, cos], [ta])
                        o.tt("pool", tb[:, 0:n, :], x2, sin[:, 0:n, :], ALU.mult, [raw, sin], [tb])
                        o.tt("dve", rot[:, 0:n, m, 0:32:2], ta[:, 0:n, :], tb[:, 0:n, :], ALU.subtract, [ta, tb], [rot])
                        o.tt("dve", ta[:, 0:n, :], x1, sin[:, 0:n, :], ALU.mult, [raw, sin], [ta])
                        o.tt("pool", tb[:, 0:n, :], x2, cos[:, 0:n, :], ALU.mult, [raw, cos], [tb])
                        o.tt("dve", rot[:, 0:n, m, 1:32:2], ta[:, 0:n, :], tb[:, 0:n, :], ALU.add, [ta, tb], [rot])
                        if isq:
                            o.red(nsq[:, 0:n, m], sq[:, 0:n, 32 * m:32 * m + 32], ALU.add, [sq], [nsq])
                            o.ts("dve", nsq[:, 0:n, m], nsq[:, 0:n, m], kbc[:, m:m + 1], None, ALU.mult, None, [nsq, kbc], [nsq])
                            o.act(nsq[:, 0:n, m], nsq[:, 0:n, m], AF.Sqrt, [nsq], [nsq])
                            o.ts("dve", rot[:, 0:n, m, 32], nsq[:, 0:n, m], -1.0, None, ALU.mult, None, [nsq], [rot])
                        else:
                            o.memset("pool", rot[:, 0:n, m, 32:33], 1.0, [rot])
                    for ci in range(n):
                        for m in range(2):
                            ps = next_ps(g)
                            o.tr(ps[0:33, 0:128], rot[:, ci, m, :], g.ident[:, :], [rot, g.ident], [ps])
                            o.copy("act" if evn % 2 == 0 else "dve", dstT[m][0:33, (c0 + ci) * 128:(c0 + ci + 1) * 128],
                                   ps[0:33, 0:128], [ps], [dstT[m]])
                            evn += 1
            lam = ph.sb([64, 4, 32], F32, "lam")
            o.dma(lam[:], g.p["da_lam"][l].partition_broadcast(64), [], [lam])
            lp = ph.sb([64, 2, 32], F32, "lp")
            o.tt("dve", lp[:, 0, :], lam[:, 0, :], lam[:, 1, :], ALU.mult, [lam], [lp])
            o.tt("dve", lp[:, 1, :], lam[:, 2, :], lam[:, 3, :], ALU.mult, [lam], [lp])
            ls = ph.sb([64, 2], F32, "ls")
            o.red(ls[:, 0:2], lp[:, :, :], ALU.add, [lp], [ls])
            o.act(ls[:], ls[:], AF.Exp, [ls], [ls])
            nlam = ph.sb([64, 1], F32, "nlam")
            o.tt("dve", nlam[:], ls[:, 1:2], ls[:, 0:1], ALU.subtract, [ls], [nlam])
            o.ts("dve", nlam[:], nlam[:], -lam_init, None, ALU.add, None, [nlam], [nlam])
            gcol = ph.sb([64, 1], F32, "gcol")
            load_cols_small(g, ph, g.p["da_norm_g"][l], 64, gcol, 0)
            o.ts("dve", gcol[:], gcol[:], 1.0 - lam_init, None, ALU.mult, None, [gcol], [gcol])
            sel = ph.sb([65, 64], F32, "sel")
            o.memset("dve", sel[:], 0.0, [sel])
            o.memset("dve", sel[64:65, :], 1.0, [sel])
            pT = [ph.sb([128, 512], BF16, "pT") for _ in range(3)]
            OA = [ph.sb([65, 512], F32, "OA") for _ in range(2)]
            rl = [ph.sb([64, 512], F32, "rl") for _ in range(2)]
            t0_ = ph.sb([64, 512], F32, "t0")
            t1_ = ph.sb([64, 512], F32, "t1")
            ob = ph.sb([64, 512], BF16, "ob")
            sps = g.ps[0:5]
            accs = g.ps[5:7]
            aux = g.ps[7]
            it = 0
            ai = 0
            for (q0, nq, isctx) in blocks_of(Ttok):
                if isctx and l == g.depth - 1:
                    continue
                keys = range(0, 2) if isctx else range(0, nch)
                for m in range(2):
                    acc = accs[ai % 2]
                    ai += 1
                    for ki, kc in enumerate(keys):
                        ps = sps[it % 5]
                        p = pT[it % 3]
                        it += 1
                        o.mm(ps[:, 0:nq], KT[m][0:33, kc * 128:(kc + 1) * 128], QT[m][0:33, q0:q0 + nq], True, True, [KT[m], QT[m]], [ps])
                        o.act(p[:, 0:nq], ps[:, 0:nq], AF.Exp, [ps], [p], scale=scale)
                        o.mm(acc[0:65, 0:nq], V1[:, kc, 0:65], p[:, 0:nq], ki == 0, ki == len(keys) - 1, [V1, p], [acc])
                    o.copy("dve", OA[m][0:65, 0:nq], acc[0:65, 0:nq], [acc], [OA[m]])
                    o.mm(aux[0:64, 0:nq], sel[0:65, 0:64], OA[m][0:65, 0:nq], True, True, [sel, OA[m]], [aux])
                    o.recip(rl[m][:, 0:nq], aux[0:64, 0:nq], [aux], [rl[m]])
                o.tt("dve", t0_[:, 0:nq], OA[0][0:64, 0:nq], rl[0][:, 0:nq], ALU.mult, [OA[0], rl[0]], [t0_])
                o.tt("pool", t1_[:, 0:nq], OA[1][0:64, 0:nq], rl[1][:, 0:nq], ALU.mult, [OA[1], rl[1]], [t1_])
                o.stt("dve", t0_[:, 0:nq], t1_[:, 0:nq], nlam[:, 0:1], t0_[:, 0:nq], ALU.mult, ALU.add, [t1_, nlam, t0_], [t0_])
                o.tt("dve", t1_[:, 0:nq], t0_[:, 0:nq], t0_[:, 0:nq], ALU.mult, [t0_], [t1_])
                o.mm(aux[0:64, 0:nq], g.ones_f[0:64, 0:64], t1_[:, 0:nq], True, True, [g.ones_f, t1_], [aux])
                o.ts("dve", t1_[:, 0:nq], aux[0:64, 0:nq], 1.0 / 64, EPS, ALU.mult, ALU.add, [aux], [t1_])
                o.act(t1_[:, 0:nq], t1_[:, 0:nq], AF.Sqrt, [t1_], [t1_])
                o.recip(t1_[:, 0:nq], t1_[:, 0:nq], [t1_], [t1_])
                o.stt("dve", ob[:, 0:nq], t0_[:, 0:nq], gcol[:, 0:1], t1_[:, 0:nq], ALU.mult, ALU.mult, [t0_, gcol, t1_], [ob])
                o.dma(g.CAT[512 + 64 * h:512 + 64 * h + 64, q0:q0 + nq], ob[:, 0:nq], [ob], [])


def make_inputs(b, x, c, ctx, c_ctx, params, final_norm_g, L, depth):
    m = {}
    m["xT"] = np.ascontiguousarray(np.concatenate([ctx[b], x[b]], axis=0).T)
    m["cvec"] = np.ascontiguousarray(np.stack([c[b], c_ctx], axis=1))
    for k in PARAM_SHAPES:
        m[k] = np.ascontiguousarray(params[k][:depth])
    m["final_norm_g"] = np.ascontiguousarray(final_norm_g)
    m.update(host_consts(L))
    return m


_CACHE = {}


def run_model(x, c, ctx, c_ctx, params, final_norm_g, depth, dbg=()):
    B, L, _ = x.shape
    key = (L, depth, tuple(dbg), tuple(sorted(MIXERS.items())))
    if key not in _CACHE:
        _CACHE[key] = build(L, depth, dbg)
    nc = _CACHE[key]
    in_maps = [make_inputs(b, x, c, ctx, c_ctx, params, final_norm_g, L, depth) for b in range(B)]
    res = run_bass_kernel_spmd(nc, in_maps, core_ids=list(range(B)))
    out = np.stack([np.ascontiguousarray(res.results[b]["yT"].T) for b in range(B)], axis=0)
    return out.astype(np.float32), res


def kernel(x, c, ctx, c_ctx, ada_w, ada_b, norm1_g, norm2_g, w_in, s5_lam_re, s5_lam_im, s5_log_step,
           s5_b_re, s5_b_im, s5_c_re, s5_c_im, s5_d, s5_glu_w, s5_glu_b, ml_conv_w, ml_conv_b, ml_gate_b,
           ml_norm_g, da_lam, da_norm_g, gla_alpha_w, gla_alpha_b, gla_norm_g, w_out, mlp_w1, mlp_w2,
           final_norm_g):
    loc = locals()
    params = {k: np.asarray(loc[k], np.float32) for k in PARAM_SHAPES}
    out, _ = run_model(np.asarray(x, np.float32), np.asarray(c, np.float32), np.asarray(ctx, np.float32),
                       np.asarray(c_ctx, np.float32), params, np.asarray(final_norm_g, np.float32), 4)
    return out
```
